# Optimizing a Trainium2 kernel written in Bass

```python
import math
import jax, jax.numpy as jnp
from jax import lax
import numpy as np

D_MODEL = 1024
BATCH = 8
SEQ = 4096
DEPTH = 2

N_HEADS = 8
HEAD_DIM = 64
V_DIM = 2 * HEAD_DIM
ATTN_WIDTH = N_HEADS * 2 * HEAD_DIM
C_CONV = D_MODEL
CONV_K = 31
D_FF = 256 * ((8 * D_MODEL // 3 + 255) // 256)
Q_BLOCK = 128
EPS = 1e-6
IN_COLS = 2 * C_CONV + 3 * ATTN_WIDTH + 2 * D_MODEL
SPLIT_POINTS = [2 * C_CONV,
                2 * C_CONV + ATTN_WIDTH,
                2 * C_CONV + 2 * ATTN_WIDTH,
                2 * C_CONV + 3 * ATTN_WIDTH,
                2 * C_CONV + 3 * ATTN_WIDTH + D_MODEL]

kernel_name = "hybrid_conformer_conv_diff_attn_macaron"


def rms_norm(x, g):
    xf = x.astype(jnp.float32)
    y = xf * lax.rsqrt(jnp.mean(xf * xf, axis=-1, keepdims=True) + EPS)
    return (y * g.astype(jnp.float32)).astype(x.dtype)


def layer_norm(x, g, b):
    xf = x.astype(jnp.float32)
    mu = jnp.mean(xf, axis=-1, keepdims=True)
    xc = xf - mu
    var = jnp.mean(xc * xc, axis=-1, keepdims=True)
    y = xc * lax.rsqrt(var + EPS) * g.astype(jnp.float32) + b.astype(jnp.float32)
    return y.astype(x.dtype)


def swiglu_ffn(h, w_in, w_out):
    a, b = jnp.split(h @ w_in, 2, axis=-1)
    return (jax.nn.silu(a) * b) @ w_out


def conv_module(u_pre, dw, dw_b, ln_g, ln_b, w_out):
    a, gte = jnp.split(u_pre, 2, axis=-1)
    u = a * jax.nn.sigmoid(gte)
    y = lax.conv_general_dilated(
        u, dw[:, None, :].astype(u.dtype), window_strides=(1,),
        padding=[(CONV_K - 1, 0)], dimension_numbers=("NWC", "WIO", "NWC"),
        feature_group_count=u.shape[-1]) + dw_b
    y = jax.nn.silu(layer_norm(y, ln_g, ln_b))
    return y @ w_out


def diff_attention(q, k, v, q_norm, k_norm, lam_q, lam_k, subln, w_o, lambda_init):
    B, T, _ = q.shape
    q = q.reshape(B, T, N_HEADS, 2, HEAD_DIM)
    k = k.reshape(B, T, N_HEADS, 2, HEAD_DIM)
    v = v.reshape(B, T, N_HEADS, V_DIM)
    q = rms_norm(q, q_norm) * (HEAD_DIM ** -0.5)
    k = rms_norm(k, k_norm)
    lq = lam_q.astype(jnp.float32)
    lk = lam_k.astype(jnp.float32)
    lam = (jnp.exp(jnp.sum(lq[0] * lk[0])) - jnp.exp(jnp.sum(lq[1] * lk[1]))
           + lambda_init)
    outs = []
    for i in range(T // Q_BLOCK):
        start = i * Q_BLOCK
        L = start + Q_BLOCK
        qb = q[:, start:L]
        kb = k[:, :L]
        vb = v[:, :L]
        s = jnp.einsum("bqhcd,bkhcd->bhcqk", qb, kb).astype(jnp.float32)
        q_pos = start + jnp.arange(Q_BLOCK)[:, None]
        k_pos = jnp.arange(L)[None, :]
        s = jnp.where(k_pos <= q_pos, s, -jnp.inf)
        p = jax.nn.softmax(s, axis=-1)
        w = p[:, :, 0] - lam * p[:, :, 1]
        outs.append(jnp.einsum("bhqk,bkhe->bqhe", w.astype(vb.dtype), vb))
    o = jnp.concatenate(outs, axis=1)
    o = rms_norm(o, subln) * (1.0 - lambda_init)
    return o.reshape(B, T, ATTN_WIDTH) @ w_o


def setup_inputs(seed: int = 0) -> dict:
    key = jax.random.key(seed)
    ks = jax.random.split(key, 24)
    f32 = jnp.float32

    def w(k, shape, fan_in):
        return jax.random.normal(k, shape, f32) * (fan_in ** -0.5)

    def gain(k, shape):
        return 1.0 + 0.02 * jax.random.normal(k, shape, f32)

    def bias(k, shape):
        return 0.02 * jax.random.normal(k, shape, f32)

    return {
        "x": jax.random.normal(ks[0], (BATCH, SEQ, D_MODEL), f32),
        "ffn1_norm": gain(ks[1], (DEPTH, D_MODEL)),
        "ffn1_w_in": w(ks[2], (DEPTH, D_MODEL, 2 * D_FF), D_MODEL),
        "ffn1_w_out": w(ks[3], (DEPTH, D_FF, D_MODEL), D_FF),
        "mix_norm": gain(ks[4], (DEPTH, D_MODEL)),
        "w_in": w(ks[5], (DEPTH, D_MODEL, IN_COLS), D_MODEL),
        "conv_dw": w(ks[6], (DEPTH, CONV_K, C_CONV), CONV_K),
        "conv_dw_b": bias(ks[7], (DEPTH, C_CONV)),
        "conv_ln_g": gain(ks[8], (DEPTH, C_CONV)),
        "conv_ln_b": bias(ks[9], (DEPTH, C_CONV)),
        "conv_w_out": w(ks[10], (DEPTH, C_CONV, D_MODEL), C_CONV),
        "q_norm": gain(ks[11], (DEPTH, HEAD_DIM)),
        "k_norm": gain(ks[12], (DEPTH, HEAD_DIM)),
        "lam_q": 0.1 * jax.random.normal(ks[13], (DEPTH, 2, HEAD_DIM), f32),
        "lam_k": 0.1 * jax.random.normal(ks[14], (DEPTH, 2, HEAD_DIM), f32),
        "attn_subln": gain(ks[15], (DEPTH, V_DIM)),
        "attn_w_out": w(ks[16], (DEPTH, ATTN_WIDTH, D_MODEL), ATTN_WIDTH),
        "w_out": w(ks[17], (DEPTH, D_MODEL, D_MODEL), D_MODEL),
        "ffn2_norm": gain(ks[18], (DEPTH, D_MODEL)),
        "ffn2_w_in": w(ks[19], (DEPTH, D_MODEL, 2 * D_FF), D_MODEL),
        "ffn2_w_out": w(ks[20], (DEPTH, D_FF, D_MODEL), D_FF),
    }


def reference(x, ffn1_norm, ffn1_w_in, ffn1_w_out, mix_norm, w_in, conv_dw, conv_dw_b,
              conv_ln_g, conv_ln_b, conv_w_out, q_norm, k_norm, lam_q, lam_k,
              attn_subln, attn_w_out, w_out, ffn2_norm, ffn2_w_in, ffn2_w_out):
    for l in range(DEPTH):
        lambda_init = 0.8 - 0.6 * math.exp(-0.3 * l)
        x = x + 0.5 * swiglu_ffn(rms_norm(x, ffn1_norm[l]), ffn1_w_in[l], ffn1_w_out[l])
        h = rms_norm(x, mix_norm[l])
        proj = h @ w_in[l]
        u_pre, q, k, v, g_conv, g_attn = jnp.split(proj, SPLIT_POINTS, axis=-1)
        y_conv = conv_module(u_pre, conv_dw[l], conv_dw_b[l], conv_ln_g[l], conv_ln_b[l],
                             conv_w_out[l])
        y_attn = diff_attention(q, k, v, q_norm[l], k_norm[l], lam_q[l], lam_k[l],
                                attn_subln[l], attn_w_out[l], lambda_init)
        merged = jax.nn.sigmoid(g_conv) * y_conv + jax.nn.sigmoid(g_attn) * y_attn
        x = x + merged @ w_out[l]
        x = x + 0.5 * swiglu_ffn(rms_norm(x, ffn2_norm[l]), ffn2_w_in[l], ffn2_w_out[l])
    return x
```

```python
import numpy as np
from contextlib import ExitStack
import concourse.bass as bass
import concourse.mybir as mybir
from concourse.bass_utils import run_bass_kernel_spmd

F32 = mybir.dt.float32
BF16 = mybir.dt.bfloat16
AF = mybir.ActivationFunctionType
ALU = mybir.AluOpType

D = 1024
KC = 8
DFF = 2816
MC = 22
T = 512
NH = 8
CK = 31
HALO = CK - 1
EPS = 1e-6
IN_COLS = 7168
N_CORES = 8
NSLOT = 6

C_F1 = 0
C_MIX = 8
C_F2 = 16
C_DW = 24
C_DWB = C_DW + 8 * CK
C_LNG = C_DWB + 8
C_LNB = C_LNG + 8
C_QN = C_LNB + 8
C_KN = C_QN + 1
C_SUB = C_KN + 1
C_LQ = C_SUB + 1
C_LK = C_LQ + 128
NCV = C_LK + 128


class Buf:
    __slots__ = ("name", "w", "r")

    def __init__(self, name):
        self.name = name
        self.w = None
        self.r = {}


class DSem:
    __slots__ = ("sem", "count", "last")

    def __init__(self, sem):
        self.sem = sem
        self.count = 0
        self.last = None


class Op:
    __slots__ = ("eng", "fn", "deps", "sig", "sem", "val", "dsem", "ndma")


class Prog:
    ENGS = ("pe", "act", "dve", "pool", "sp")

    def __init__(self):
        self.ops = {e: [] for e in self.ENGS}
        self.dsems = []

    def add(self, eng, fn, reads=(), writes=(), dsem=None, ndma=1):
        op = Op()
        op.eng = eng
        op.fn = fn
        op.dsem = dsem
        op.ndma = ndma
        op.sig = dsem is not None
        op.sem = None
        op.val = 0
        deps = set()
        for b in reads:
            if b.w is not None:
                deps.add(b.w)
        for b in writes:
            if b.w is not None:
                deps.add(b.w)
            for o in b.r.values():
                deps.add(o)
        if dsem is not None and dsem.last is not None:
            deps.add(dsem.last)
        if eng == "pe":
            deps = {d for d in deps if not (d.eng == "pe" and d.dsem is None)}
        op.deps = list(deps)
        for d in op.deps:
            d.sig = True
        rkey = eng if dsem is None else ("dma", id(op))
        for b in reads:
            b.r[rkey] = op
        for b in writes:
            b.w = op
            b.r = {}
        if dsem is not None:
            dsem.last = op
        self.ops[eng].append(op)
        return op

    def emit(self, nc, block, esem):
        for e in self.ENGS:
            cnt = 0
            for op in self.ops[e]:
                if op.dsem is not None:
                    op.dsem.count += 16 * op.ndma
                    op.sem = op.dsem.sem
                    op.val = op.dsem.count
                elif op.sig:
                    cnt += 1
                    op.sem = esem[e]
                    op.val = cnt

        def run(engine, e, final=None):
            known = {}
            for op in self.ops[e]:
                need = {}
                for d in op.deps:
                    k = id(d.sem)
                    if k not in need or need[k][1] < d.val:
                        need[k] = (d.sem, d.val)
                for k, (sem, val) in need.items():
                    if known.get(k, 0) < val:
                        engine.wait_ge(sem, val)
                        known[k] = val
                res = op.fn(engine)
                if op.dsem is not None:
                    if not isinstance(res, (list, tuple)):
                        res = [res]
                    assert len(res) == op.ndma
                    for ins in res:
                        ins.then_inc(op.dsem.sem, 16)
                elif op.sig:
                    res.then_inc(op.sem, 1)
            if final is not None:
                final(engine)

        def final_sp(engine):
            for ds in self.dsems:
                if ds.count > 0:
                    engine.wait_ge(ds.sem, ds.count)

        @block.tensor
        def _(eng):
            run(eng, "pe")

        @block.scalar
        def _(eng):
            run(eng, "act")

        @block.vector
        def _(eng):
            run(eng, "dve")

        @block.gpsimd
        def _(eng):
            run(eng, "pool")

        @block.sync
        def _(eng):
            run(eng, "sp", final_sp)


WNAMES = ["ffn1_w_in", "ffn1_w_out", "w_in", "conv_w_out", "attn_w_out", "w_out", "ffn2_w_in", "ffn2_w_out"]
WSHAPES = {"ffn1_w_in": (D, 2 * DFF), "ffn1_w_out": (DFF, D), "w_in": (D, IN_COLS), "conv_w_out": (D, D),
           "attn_w_out": (D, D), "w_out": (D, D), "ffn2_w_in": (D, 2 * DFF), "ffn2_w_out": (DFF, D)}


def build_nc(S, depth, dbg=None):
    NT = S // T
    nc = bass.Bass("TRN2", target_bir_lowering=False)
    xT_in = nc.dram_tensor("xT", [D, S], F32, kind="ExternalInput").ap()
    cvec_in = nc.dram_tensor("cvec", [128, depth * NCV], F32, kind="ExternalInput").ap()
    tri_in = nc.dram_tensor("tri", [128, 256], F32, kind="ExternalInput").ap()
    W = {}
    for n in WNAMES:
        W[n] = nc.dram_tensor(n, [depth, WSHAPES[n][0], WSHAPES[n][1]], F32, kind="ExternalInput").ap()
    yT_out = nc.dram_tensor("yT", [D, S], F32, kind="ExternalOutput").ap()
    xmid = [nc.dram_tensor(f"xmid{l}", [D, S], F32, kind="Internal").ap() for l in range(max(depth - 1, 1))]
    kts = [nc.dram_tensor(f"kts{l}", [NH, 128, S], BF16, kind="Internal").ap() for l in range(depth)]
    vs = [nc.dram_tensor(f"vs{l}", [NH, S, 128], BF16, kind="Internal").ap() for l in range(depth)]

    P = Prog()
    es = ExitStack()
    with es:
        def sb(name, shape, dt):
            return es.enter_context(nc.sbuf_tensor(name, shape, dt))

        def mksem(name):
            return es.enter_context(nc.semaphore(name))

        def dsem(name):
            d = DSem(mksem(name))
            P.dsems.append(d)
            return d

        esem = {e: mksem("e_" + e) for e in Prog.ENGS}

        xT = sb("xT_sb", [128, KC, T], F32)
        xT_b = [Buf(f"xT{c}") for c in range(KC)]
        xT_ds = [dsem(f"xT_ds{c}") for c in range(KC)]
        hT = sb("hT", [128, KC, T], BF16)
        hT_b = [Buf(f"hT{c}") for c in range(KC)]
        arena = sb("arena", [128, 25 * T], BF16)
        ar_b = [Buf(f"ar{m}") for m in range(25)]
        cT = sb("cT", [128, KC, T], BF16)
        uh = sb("uh", [128, KC, HALO], BF16)
        uh_b = [Buf(f"uh{c}") for c in range(KC)]
        cT_b = [Buf(f"cT{c}") for c in range(KC)]
        qT0 = sb("qT0", [128, NH, T], BF16)
        qT1 = sb("qT1", [128, NH, T], BF16)
        qT_b = [Buf(f"qT{j}") for j in range(NH)]
        aoT = sb("aoT", [128, NH, T], BF16)
        aoT_b = [Buf(f"aoT{j}") for j in range(NH)]
        oc = [sb(f"oc{i}", [128, T], F32) for i in range(2)]
        oc_b = [Buf(f"oc{i}") for i in range(2)]
        rl = [sb(f"rl{i}", [128, T], F32) for i in range(2)]
        rl_b = [Buf(f"rl{i}") for i in range(2)]
        ob = [sb(f"ob{i}", [128, T], F32) for i in range(2)]
        ob_b = [Buf(f"ob{i}") for i in range(2)]
        kn = [sb(f"kn{i}", [128, T], BF16) for i in range(2)]
        kn_b = [Buf(f"kn{i}") for i in range(2)]
        kn_ds = [dsem(f"kn_ds{i}") for i in range(2)]
        vt = [sb(f"vt{i}", [128, D], BF16) for i in range(2)]
        vt_b = [Buf(f"vt{i}") for i in range(2)]
        vt_ds = [dsem(f"vt_ds{i}") for i in range(2)]
        KTj = [sb(f"KTj{i}", [128, S], BF16) for i in range(2)]
        KTj_b = [Buf(f"KTj{i}") for i in range(2)]
        KTj_ds = [dsem(f"KTj_ds{i}") for i in range(2)]
        Vj = [sb(f"Vj{i}", [128, S // 128, 128], BF16) for i in range(2)]
        Vj_b = [Buf(f"Vj{i}") for i in range(2)]
        Vj_ds = [dsem(f"Vj_ds{i}") for i in range(2)]
        NPT = 3
        pt = [sb(f"pt{i}", [128, T], BF16) for i in range(NPT)]
        pt_b = [Buf(f"pt{i}") for i in range(NPT)]
        NTMP = 6
        tmp = [sb(f"tmp{i}", [128, T], F32) for i in range(NTMP)]
        tmp_b = [Buf(f"tmp{i}") for i in range(NTMP)]
        sqb = [sb(f"sqb{i}", [128, T], BF16) for i in range(3)]
        sqb_b = [Buf(f"sqb{i}") for i in range(3)]
        NTB = 3
        tmpb = [sb(f"tmpb{i}", [128, T], BF16) for i in range(NTB)]
        tmpb_b = [Buf(f"tmpb{i}") for i in range(NTB)]
        wslot = [sb(f"wslot{i}", [128, KC * T], BF16) for i in range(NSLOT)]
        wslot_b = [Buf(f"wslot{i}") for i in range(NSLOT)]
        wslot_ds = [dsem(f"wslot_ds{i}") for i in range(NSLOT)]
        cv = sb("cv", [128, depth * NCV], F32)
        cv_b = Buf("cv")
        cv_ds = dsem("cv_ds")
        tri = sb("tri_sb", [128, 256], BF16)
        tri_b = Buf("tri")
        tri_ds = dsem("tri_ds")
        ones_d = sb("ones_d", [128, 128], BF16)
        ones_h = sb("ones_h", [128, 128], BF16)
        ones_1 = sb("ones_1", [128, 128], BF16)
        ones_q = sb("ones_q", [128, 128], BF16)
        const_b = Buf("consts")
        epsc = sb("epsc", [128, 1], F32)
        lay = sb("lay", [128, depth * 8], F32)
        lay_b = Buf("lay")
        lamt = sb("lamt", [128, 128], F32)
        lamt_b = Buf("lamt")
        lam2 = sb("lam2", [128, 4], F32)
        lam2_b = Buf("lam2")

        ps = [es.enter_context(nc.psum_tensor(f"ps{i}", [128, T], F32)) for i in range(8)]
        ps_b = [Buf(f"ps{i}") for i in range(8)]

        class Rot:
            def __init__(self, n):
                self.n = n
                self.i = 0

            def next(self):
                k = self.i % self.n
                self.i += 1
                return k

        tmp_rot = Rot(NTMP)
        tmpb_rot = Rot(NTB)
        slot_rot = Rot(NSLOT)

        P.add("sp", lambda e: e.dma_start(out=cv[:], in_=cvec_in), writes=[cv_b], dsem=cv_ds)
        P.add("pool", lambda e: e.dma_start(out=tri[:], in_=tri_in), writes=[tri_b], dsem=tri_ds)
        P.add("dve", lambda e: e.memset(ones_d[:], 1.0 / D), writes=[const_b])
        P.add("dve", lambda e: e.memset(ones_h[:], 1.0 / 128), writes=[const_b])
        P.add("dve", lambda e: e.memset(ones_1[:], 1.0), writes=[const_b])
        P.add("dve", lambda e: e.memset(ones_q[:], 0.0), writes=[const_b])
        P.add("dve", lambda e: e.memset(ones_q[0:64, 0:64], 1.0 / 64), writes=[const_b])
        P.add("dve", lambda e: e.memset(ones_q[64:128, 64:128], 1.0 / 64), writes=[const_b])
        P.add("dve", lambda e: e.memset(epsc[:], EPS), writes=[const_b])
        P.add("dve", lambda e: e.memset(qT0[:], 0.0), writes=qT_b)
        P.add("dve", lambda e: e.memset(qT1[:], 0.0), writes=qT_b)

        def cvcol(l, col, n=1):
            return cv[:, l * NCV + col: l * NCV + col + n]

        fence = []

        def wload(src_ap, shape):
            k = slot_rot.next()
            a, b = shape
            dst = wslot[k][:, 0:a * b].rearrange("p (a b) -> p a b", a=a)
            rd = list(fence)
            del fence[:]
            P.add("pool", lambda e, dst=dst, src=src_ap: e.dma_start(out=dst, in_=src),
                  reads=rd, writes=[wslot_b[k]], dsem=wslot_ds[k])
            return dst, wslot_b[k]

        def wview(name, l):
            return W[name][l].rearrange("(kc p) n -> p kc n", p=128)

        def mm(out, lhsT, rhs, start, stop, reads, writes):
            P.add("pe", lambda e: e.matmul(out, lhsT=lhsT, rhs=rhs, start=start, stop=stop), reads=reads, writes=writes)

        def rstd_from_mean(ms_ps, ms_buf, n=T):
            k = tmp_rot.next()
            P.add("act", lambda e: e.activation(out=tmp[k][:, 0:n], in_=ms_ps, func=AF.Ln, bias=epsc[:, 0:1], scale=1.0),
                  reads=[ms_buf, const_b], writes=[tmp_b[k]])
            P.add("act", lambda e: e.activation(out=tmp[k][:, 0:n], in_=tmp[k][:, 0:n], func=AF.Exp, scale=-0.5),
                  reads=[tmp_b[k]], writes=[tmp_b[k]])
            return k

        NBANK = 0
        sq_rot = Rot(3)
        norm_pending = []

        def norm_feed(c, lag):
            kb_ = sq_rot.next()
            P.add("act", lambda e: e.activation(out=sqb[kb_][:], in_=xT[:, c, :], func=AF.Square),
                  reads=[xT_b[c]], writes=[sqb_b[kb_]])

            def stats():
                mm(ps[NBANK][:], ones_d[:], sqb[kb_][:], c == 0, c == KC - 1, [sqb_b[kb_], const_b], [ps_b[NBANK]])
            norm_pending.append(stats)
            while len(norm_pending) > lag:
                norm_pending.pop(0)()

        def norm_flush():
            while norm_pending:
                norm_pending.pop(0)()

        def norm_finish(l, gcol):
            norm_flush()
            kr = rstd_from_mean(ps[NBANK][:], ps_b[NBANK])
            for c in range(KC):
                P.add("dve", lambda e, c=c: e.scalar_tensor_tensor(out=hT[:, c, :], in0=xT[:, c, :],
                                                                  scalar=cvcol(l, gcol + c), in1=tmp[kr][:],
                                                                  op0=ALU.mult, op1=ALU.mult),
                      reads=[xT_b[c], tmp_b[kr], cv_b], writes=[hT_b[c]])

        hid = arena[:, 0:MC * T].rearrange("p (m t) -> p m t", m=MC)
        mg = arena[:, 0:KC * T].rearrange("p (m t) -> p m t", m=KC)
        mg_b = ar_b[0:KC]

        def ffn(l, gcol, w_in_name, w_out_name, post=None, feed=True):
            norm_finish(l, gcol)
            wv = wview(w_in_name, l)
            wo = W[w_out_name][l].rearrange("(m p) n -> p m n", p=128)
            ngrp = (MC + 3) // 4
            for g in range(ngrp):
                nm = min(4, MC - 4 * g)
                wa, wa_b = wload(wv[:, :, 512 * g: 512 * g + 128 * nm], (KC, 128 * nm))
                wb, wb_b = wload(wv[:, :, DFF + 512 * g: DFF + 512 * g + 128 * nm], (KC, 128 * nm))
                for mi in range(nm):
                    m = 4 * g + mi
                    ba = (2 * m) % 4
                    bb = (2 * m + 1) % 4
                    for c in range(KC):
                        mm(ps[ba][:], wa[:, c, 128 * mi:128 * (mi + 1)], hT[:, c, :], c == 0, c == KC - 1,
                           [wa_b, hT_b[c]], [ps_b[ba]])
                    for c in range(KC):
                        mm(ps[bb][:], wb[:, c, 128 * mi:128 * (mi + 1)], hT[:, c, :], c == 0, c == KC - 1,
                           [wb_b, hT_b[c]], [ps_b[bb]])
                    k = tmp_rot.next()
                    P.add("act", lambda e, k=k, ba=ba: e.activation(out=tmp[k][:], in_=ps[ba][:], func=AF.Silu),
                          reads=[ps_b[ba]], writes=[tmp_b[k]])
                    P.add("dve", lambda e, k=k, bb=bb, m=m: e.tensor_tensor(out=hid[:, m, :], in0=ps[bb][:], in1=tmp[k][:],
                                                                           op=ALU.mult),
                          reads=[ps_b[bb], tmp_b[k]], writes=[ar_b[m]])
            for d in range(KC):
                wd, wd_b = wload(wo[:, :, 128 * d:128 * (d + 1)], (MC, 128))
                bk = 4 + (d % 4)
                for m in range(MC):
                    mm(ps[bk][:], wd[:, m, :], hid[:, m, :], m == 0, m == MC - 1, [wd_b, ar_b[m]], [ps_b[bk]])
                P.add("dve", lambda e, d=d, bk=bk: e.scalar_tensor_tensor(out=xT[:, d, :], in0=ps[bk][:], scalar=0.5,
                                                                         in1=xT[:, d, :], op0=ALU.mult, op1=ALU.add),
                      reads=[ps_b[bk], xT_b[d]], writes=[xT_b[d]])
                if post is not None:
                    post(d)
                if feed:
                    norm_feed(d, 2)

        UW = 544
        u_all = arena[:, 0:KC * UW].rearrange("p (c w) -> p c w", c=KC)
        Y0 = 9 * T
        y_all = arena[:, Y0:Y0 + 2 * KC * T].bitcast(F32).rearrange("p (c t) -> p c t", c=KC)

        def u_bufs(c):
            lo = (c * UW * 2) // 1024
            hi = ((c + 1) * UW * 2 - 1) // 1024
            return [ar_b[i] for i in range(lo, hi + 1)]

        def y_bufs(c):
            return [ar_b[9 + 2 * c], ar_b[10 + 2 * c]]

        def layer_setup(l):
            lam_init = 0.8 - 0.6 * float(np.exp(-0.3 * l))
            P.add("dve", lambda e: e.tensor_single_scalar(out=lay[:, l * 8 + 0:l * 8 + 1], in_=cvcol(l, C_QN), scalar=0.125, op=ALU.mult),
                  reads=[cv_b], writes=[lay_b])
            P.add("dve", lambda e: e.tensor_single_scalar(out=lay[:, l * 8 + 1:l * 8 + 2], in_=cvcol(l, C_SUB), scalar=1.0 - lam_init, op=ALU.mult),
                  reads=[cv_b], writes=[lay_b])
            P.add("dve", lambda e: e.tensor_tensor(out=lamt[:], in0=cvcol(l, C_LQ, 128), in1=cvcol(l, C_LK, 128), op=ALU.mult),
                  reads=[cv_b], writes=[lamt_b])
            P.add("dve", lambda e: e.reduce_sum(out=lam2[:, 0:2], in_=lamt[:].rearrange("p (a b) -> p a b", a=2),
                                                axis=mybir.AxisListType.X),
                  reads=[lamt_b], writes=[lam2_b])
            P.add("act", lambda e: e.activation(out=lam2[:, 2:4], in_=lam2[:, 0:2], func=AF.Exp), reads=[lam2_b], writes=[lam2_b])
            P.add("dve", lambda e: e.scalar_tensor_tensor(out=lay[:, l * 8 + 2:l * 8 + 3], in0=lam2[:, 3:4], scalar=-lam_init,
                                                         in1=lam2[:, 2:3], op0=ALU.add, op1=ALU.subtract),
                  reads=[lam2_b], writes=[lay_b])

        def mixer(l, i):
            wv = wview("w_in", l)
            norm_finish(l, C_MIX)
            tok0 = i * T
            wvb = [wload(wv[:, :, 4096 + 512 * f: 4096 + 512 * (f + 1)], (KC, 512)) for f in range(2)]
            for tc in range(T // 128):
                r = tc % 2
                for f in range(2):
                    bk = (2 * tc + f) % 4
                    for c in range(KC):
                        mm(ps[bk][:], hT[:, c, 128 * tc:128 * (tc + 1)], wvb[f][0][:, c, :], c == 0, c == KC - 1,
                           [hT_b[c], wvb[f][1]], [ps_b[bk]])
                    P.add("act", lambda e, r=r, f=f, bk=bk: e.activation(out=vt[r][:, 512 * f:512 * (f + 1)], in_=ps[bk][:], func=AF.Copy),
                          reads=[ps_b[bk]], writes=[vt_b[r]])
                dst = vs[l][:, tok0 + 128 * tc: tok0 + 128 * (tc + 1), :].rearrange("h t e -> t h e")
                src = vt[r][:].rearrange("p (h e) -> p h e", h=NH)
                P.add("sp", lambda e, dst=dst, src=src: e.dma_start(out=dst, in_=src), reads=[vt_b[r]],
                      writes=[vs_b[l][i]], dsem=vt_ds[r])
            pending = None
            for which in ("k", "q"):
                base = 3072 if which == "k" else 2048
                blocks = [wload(wv[:, :, base + 512 * f: base + 512 * (f + 1)], (KC, 512)) for f in range(2)]
                for j in range(NH):
                    wblk, wblk_b = blocks[j // 4]
                    bk = j % 4
                    for c in range(KC):
                        mm(ps[bk][:], wblk[:, c, 128 * (j % 4):128 * (j % 4 + 1)], hT[:, c, :], c == 0, c == KC - 1,
                           [wblk_b, hT_b[c]], [ps_b[bk]])
                    ksq = tmpb_rot.next()
                    P.add("act", lambda e, ksq=ksq, bk=bk: e.activation(out=tmpb[ksq][:], in_=ps[bk][:], func=AF.Square),
                          reads=[ps_b[bk]], writes=[tmpb_b[ksq]])

                    def tail(which=which, j=j, bk=bk, ksq=ksq):
                        b2 = 4 + (j % 2)
                        mm(ps[b2][:], ones_q[:], tmpb[ksq][:], True, True, [tmpb_b[ksq], const_b], [ps_b[b2]])
                        kr = rstd_from_mean(ps[b2][:], ps_b[b2])
                        if which == "k":
                            r = j % 2
                            P.add("dve", lambda e: e.scalar_tensor_tensor(
                                out=kn[r][:], in0=ps[bk][:], scalar=cvcol(l, C_KN), in1=tmp[kr][:], op0=ALU.mult, op1=ALU.mult),
                                reads=[ps_b[bk], tmp_b[kr], cv_b], writes=[kn_b[r]])
                            P.add("sp", lambda e: e.dma_start(out=kts[l][j, :, tok0:tok0 + T], in_=kn[r][:]),
                                  reads=[kn_b[r]], writes=[kts_b[l][j][i]], dsem=kn_ds[r])
                        else:
                            gq = lay[:, l * 8:l * 8 + 1]
                            P.add("dve", lambda e: e.scalar_tensor_tensor(
                                out=qT0[0:64, j, :], in0=ps[bk][0:64, :], scalar=gq[0:64, :], in1=tmp[kr][0:64, :],
                                op0=ALU.mult, op1=ALU.mult),
                                reads=[ps_b[bk], tmp_b[kr], lay_b], writes=[qT_b[j]])
                            P.add("dve", lambda e: e.scalar_tensor_tensor(
                                out=qT1[64:128, j, :], in0=ps[bk][64:128, :], scalar=gq[64:128, :], in1=tmp[kr][64:128, :],
                                op0=ALU.mult, op1=ALU.mult),
                                reads=[ps_b[bk], tmp_b[kr], lay_b], writes=[qT_b[j]])
                    if pending is not None:
                        pending()
                    pending = tail
            if pending is not None:
                pending()
                pending = None
            if i == 0:
                P.add("dve", lambda e: e.memset(u_all[:, :, 0:HALO], 0.0), writes=[ar_b[k_] for k_ in range(9)])
            else:
                for cc in range(KC):
                    P.add("act", lambda e, cc=cc: e.activation(out=u_all[:, cc, 0:HALO], in_=uh[:, cc, :], func=AF.Copy),
                          reads=[uh_b[cc]], writes=u_bufs(cc))
            ablk = [wload(wv[:, :, 512 * f: 512 * (f + 1)], (KC, 512)) for f in range(2)]
            gblk = [wload(wv[:, :, 1024 + 512 * f: 1024 + 512 * (f + 1)], (KC, 512)) for f in range(2)]
            for cc in range(KC):
                ba = (2 * cc) % 4
                bg = (2 * cc + 1) % 4
                wa, wa_b = ablk[cc // 4]
                wg, wg_b = gblk[cc // 4]
                for c in range(KC):
                    mm(ps[bg][:], wg[:, c, 128 * (cc % 4):128 * (cc % 4 + 1)], hT[:, c, :], c == 0, c == KC - 1,
                       [wg_b, hT_b[c]], [ps_b[bg]])
                for c in range(KC):
                    mm(ps[ba][:], wa[:, c, 128 * (cc % 4):128 * (cc % 4 + 1)], hT[:, c, :], c == 0, c == KC - 1,
                       [wa_b, hT_b[c]], [ps_b[ba]])
                k = tmp_rot.next()
                P.add("act", lambda e, k=k, bg=bg: e.activation(out=tmp[k][:], in_=ps[bg][:], func=AF.Sigmoid),
                      reads=[ps_b[bg]], writes=[tmp_b[k]])
                P.add("dve", lambda e, k=k, ba=ba, cc=cc: e.tensor_tensor(out=u_all[:, cc, HALO:HALO + T], in0=ps[ba][:], in1=tmp[k][:],
                                                                         op=ALU.mult),
                      reads=[ps_b[ba], tmp_b[k]], writes=u_bufs(cc))

            nkb = 4 * (i + 1)
            L = nkb * 128
            cg = conv_gen(l)
            NCONV_HEADS = 7
            quota = (KC * CK + NCONV_HEADS - 1) // NCONV_HEADS

            def pump(k):
                for _ in range(k):
                    if next(cg, "done") == "done":
                        return

            carry = None
            for j in range(NH):
                carry = attention_head(l, i, j, nkb, L, carry, pump if j < NCONV_HEADS else None, quota)
                if j == NCONV_HEADS - 1:
                    pump(KC * CK + 8)
                    conv_ln(l)
            carry(0)
            merge_and_out(l)

        def attention_head(l, i, j, nkb, L, carry, pump, quota):
            s = j % 2
            P.add("sp", lambda e: e.dma_start(out=KTj[s][:, 0:L], in_=kts[l][j, :, 0:L]),
                  reads=[kts_b[l][j][t_] for t_ in range(i + 1)], writes=[KTj_b[s]], dsem=KTj_ds[s])
            P.add("sp", lambda e: e.dma_start(out=Vj[s][:, 0:nkb, :], in_=vs[l][j, 0:L, :].rearrange("(n p) e -> p n e", p=128)),
                  reads=[vs_b[l][t_] for t_ in range(i + 1)], writes=[Vj_b[s]], dsem=Vj_ds[s])
            blocks = [(kb, c) for kb in range(nkb) for c in range(2)]
            OB = [4, 5]
            LB = [6, 7]

            def s_mm(n):
                kb, c = blocks[n]
                r = kb - 4 * i
                c0 = 128 * r if r > 0 else 0
                sbk = n % 4
                q = (qT0 if c == 0 else qT1)
                mm(ps[sbk][:, c0:T], KTj[s][:, 128 * kb:128 * (kb + 1)], q[:, j, c0:T], True, r < 0,
                   [KTj_b[s], qT_b[j]], [ps_b[sbk]])
                if r >= 0:
                    mm(ps[sbk][:, c0:c0 + 128], tri[:, 128:256], tri[:, 0:128], False, True, [tri_b], [ps_b[sbk]])
                p_ = n % NPT
                P.add("act", lambda e: e.activation(out=pt[p_][:, c0:T], in_=ps[sbk][:, c0:T], func=AF.Exp),
                      reads=[ps_b[sbk]], writes=[pt_b[p_]])

            def pv_mm(n):
                kb, c = blocks[n]
                r = kb - 4 * i
                c0 = 128 * r if r > 0 else 0
                p_ = n % NPT
                mm(ps[OB[c]][:, c0:T], Vj[s][:, kb, :], pt[p_][:, c0:T], kb == 0, kb == nkb - 1,
                   [Vj_b[s], pt_b[p_]], [ps_b[OB[c]]])
                mm(ps[LB[c]][:, c0:T], ones_1[:], pt[p_][:, c0:T], kb == 0, kb == nkb - 1,
                   [const_b, pt_b[p_]], [ps_b[LB[c]]])

            nb = len(blocks)
            s_mm(0)
            if nb > 1:
                s_mm(1)
            for n in range(nb):
                if n + 2 < nb:
                    s_mm(n + 2)
                pv_mm(n)
                if pump is not None:
                    k_ = (quota * (n + 1)) // nb - (quota * n) // nb
                    pump(k_)
                if n == 5 and carry is not None:
                    carry((n + 3) % 4)
                    carry = None
            if carry is not None:
                carry(0)
            for c in range(2):
                P.add("act", lambda e, c=c: e.activation(out=rl[c][:], in_=ps[LB[c]][:], func=AF.Ln),
                      reads=[ps_b[LB[c]]], writes=[rl_b[c]])
                P.add("dve", lambda e, c=c: e.tensor_copy(out=oc[c][:], in_=ps[OB[c]][:]),
                      reads=[ps_b[OB[c]]], writes=[oc_b[c]])
            a = []
            for c in range(2):
                P.add("act", lambda e, c=c: e.activation(out=rl[c][:], in_=rl[c][:], func=AF.Exp, scale=-1.0),
                      reads=[rl_b[c]], writes=[rl_b[c]])
                ka = tmp_rot.next()
                P.add("dve", lambda e, ka=ka, c=c: e.tensor_tensor(out=tmp[ka][:], in0=oc[c][:], in1=rl[c][:], op=ALU.mult),
                      reads=[oc_b[c], rl_b[c]], writes=[tmp_b[ka]])
                a.append(ka)
            ko = j % 2
            P.add("dve", lambda e: e.scalar_tensor_tensor(out=ob[ko][:], in0=tmp[a[1]][:], scalar=lay[:, l * 8 + 2:l * 8 + 3],
                                                         in1=tmp[a[0]][:], op0=ALU.mult, op1=ALU.add),
                  reads=[tmp_b[a[0]], tmp_b[a[1]], lay_b], writes=[ob_b[ko]])

            def fin2(sbk):
                ksq = tmpb_rot.next()
                P.add("act", lambda e: e.activation(out=tmpb[ksq][:], in_=ob[ko][:], func=AF.Square),
                      reads=[ob_b[ko]], writes=[tmpb_b[ksq]])
                mm(ps[sbk][:], ones_h[:], tmpb[ksq][:], True, True, [tmpb_b[ksq], const_b], [ps_b[sbk]])
                kr = rstd_from_mean(ps[sbk][:], ps_b[sbk])
                P.add("dve", lambda e: e.scalar_tensor_tensor(out=aoT[:, j, :], in0=ob[ko][:], scalar=lay[:, l * 8 + 1:l * 8 + 2],
                                                             in1=tmp[kr][:], op0=ALU.mult, op1=ALU.mult),
                      reads=[ob_b[ko], tmp_b[kr], lay_b], writes=[aoT_b[j]])
            return fin2

        def conv_gen(l):
            for c0 in range(0, KC, 2):
                for k in range(CK):
                    for c in (c0, c0 + 1):
                        yb = y_bufs(c)
                        ub = u_bufs(c)
                        if k == 0:
                            P.add("dve", lambda e, c=c: e.tensor_scalar(out=y_all[:, c, :], in0=u_all[:, c, 0:T],
                                                                       scalar1=cvcol(l, C_DW + c * CK), scalar2=cvcol(l, C_DWB + c),
                                                                       op0=ALU.mult, op1=ALU.add),
                                  reads=ub + [cv_b], writes=yb)
                        else:
                            P.add("dve", lambda e, c=c, k=k: e.scalar_tensor_tensor(out=y_all[:, c, :], in0=u_all[:, c, k:k + T],
                                                                              scalar=cvcol(l, C_DW + c * CK + k), in1=y_all[:, c, :],
                                                                              op0=ALU.mult, op1=ALU.add),
                                  reads=ub + yb + [cv_b], writes=yb)
                        yield
                for c in (c0, c0 + 1):
                    P.add("act", lambda e, c=c: e.activation(out=uh[:, c, :], in_=u_all[:, c, T:T + HALO], func=AF.Copy),
                          reads=u_bufs(c), writes=[uh_b[c]])

        def conv_ln(l):
            for c in range(KC):
                k1 = tmpb_rot.next()
                k2 = tmpb_rot.next()
                P.add("act", lambda e, c=c, k1=k1: e.activation(out=tmpb[k1][:], in_=y_all[:, c, :], func=AF.Copy),
                      reads=y_bufs(c), writes=[tmpb_b[k1]])
                P.add("act", lambda e, c=c, k2=k2: e.activation(out=tmpb[k2][:], in_=y_all[:, c, :], func=AF.Square),
                      reads=y_bufs(c), writes=[tmpb_b[k2]])
                mm(ps[0][:], ones_d[:], tmpb[k1][:], c == 0, c == KC - 1, [tmpb_b[k1], const_b], [ps_b[0]])
                mm(ps[1][:], ones_d[:], tmpb[k2][:], c == 0, c == KC - 1, [tmpb_b[k2], const_b], [ps_b[1]])
            kmean = tmp_rot.next()
            kvar = tmp_rot.next()
            P.add("act", lambda e: e.activation(out=tmp[kmean][:], in_=ps[0][:], func=AF.Copy), reads=[ps_b[0]], writes=[tmp_b[kmean]])
            P.add("dve", lambda e: e.tensor_tensor(out=tmp[kvar][:], in0=tmp[kmean][:], in1=tmp[kmean][:], op=ALU.mult),
                  reads=[tmp_b[kmean]], writes=[tmp_b[kvar]])
            P.add("dve", lambda e: e.tensor_tensor(out=tmp[kvar][:], in0=ps[1][:], in1=tmp[kvar][:], op=ALU.subtract),
                  reads=[ps_b[1], tmp_b[kvar]], writes=[tmp_b[kvar]])
            P.add("dve", lambda e: e.tensor_single_scalar(out=tmp[kvar][:], in_=tmp[kvar][:], scalar=0.0, op=ALU.max),
                  reads=[tmp_b[kvar]], writes=[tmp_b[kvar]])
            kr = rstd_from_mean(tmp[kvar][:], tmp_b[kvar])
            knm = tmp_rot.next()
            P.add("dve", lambda e: e.scalar_tensor_tensor(out=tmp[knm][:], in0=tmp[kmean][:], scalar=-1.0, in1=tmp[kr][:],
                                                         op0=ALU.mult, op1=ALU.mult),
                  reads=[tmp_b[kmean], tmp_b[kr]], writes=[tmp_b[knm]])
            for c in range(KC):
                P.add("dve", lambda e, c=c: e.tensor_tensor(out=y_all[:, c, :], in0=y_all[:, c, :], in1=tmp[kr][:], op=ALU.mult),
                      reads=y_bufs(c) + [tmp_b[kr]], writes=y_bufs(c))
                P.add("dve", lambda e, c=c: e.tensor_tensor(out=y_all[:, c, :], in0=y_all[:, c, :], in1=tmp[knm][:], op=ALU.add),
                      reads=y_bufs(c) + [tmp_b[knm]], writes=y_bufs(c))
                P.add("act", lambda e, c=c: e.activation(out=cT[:, c, :], in_=y_all[:, c, :], func=AF.Silu,
                                                        bias=cvcol(l, C_LNB + c), scale=cvcol(l, C_LNG + c)),
                      reads=y_bufs(c) + [cv_b], writes=[cT_b[c]])

        def merge_and_out(l):
            wv = wview("w_in", l)
            wc_v = wview("conv_w_out", l)
            wa_v = wview("attn_w_out", l)
            wo_v = wview("w_out", l)
            for f in range(2):
                gcb, gcb_b = wload(wv[:, :, 5120 + 512 * f: 5120 + 512 * (f + 1)], (KC, 512))
                wcb, wcb_b = wload(wc_v[:, :, 512 * f:512 * (f + 1)], (KC, 512))
                gab, gab_b = wload(wv[:, :, 6144 + 512 * f: 6144 + 512 * (f + 1)], (KC, 512))
                wab, wab_b = wload(wa_v[:, :, 512 * f:512 * (f + 1)], (KC, 512))
                for dd in range(4):
                    d = 4 * f + dd
                    cs = slice(128 * dd, 128 * (dd + 1))
                    for c in range(KC):
                        mm(ps[0][:], gcb[:, c, cs], hT[:, c, :], c == 0, c == KC - 1, [gcb_b, hT_b[c]], [ps_b[0]])
                    for c in range(KC):
                        mm(ps[1][:], wcb[:, c, cs], cT[:, c, :], c == 0, c == KC - 1, [wcb_b, cT_b[c]], [ps_b[1]])
                    for c in range(KC):
                        mm(ps[2][:], gab[:, c, cs], hT[:, c, :], c == 0, c == KC - 1, [gab_b, hT_b[c]], [ps_b[2]])
                    for c in range(KC):
                        mm(ps[3][:], wab[:, c, cs], aoT[:, c, :], c == 0, c == KC - 1, [wab_b, aoT_b[c]], [ps_b[3]])
                    k1 = tmp_rot.next()
                    k2 = tmp_rot.next()
                    P.add("act", lambda e, k1=k1: e.activation(out=tmp[k1][:], in_=ps[0][:], func=AF.Sigmoid),
                          reads=[ps_b[0]], writes=[tmp_b[k1]])
                    P.add("act", lambda e, k2=k2: e.activation(out=tmp[k2][:], in_=ps[2][:], func=AF.Sigmoid),
                          reads=[ps_b[2]], writes=[tmp_b[k2]])
                    P.add("dve", lambda e, k1=k1: e.tensor_tensor(out=tmp[k1][:], in0=ps[1][:], in1=tmp[k1][:], op=ALU.mult),
                          reads=[ps_b[1], tmp_b[k1]], writes=[tmp_b[k1]])
                    P.add("dve", lambda e, k2=k2: e.tensor_tensor(out=tmp[k2][:], in0=ps[3][:], in1=tmp[k2][:], op=ALU.mult),
                          reads=[ps_b[3], tmp_b[k2]], writes=[tmp_b[k2]])
                    P.add("dve", lambda e, k1=k1, k2=k2, d=d: e.tensor_tensor(out=mg[:, d, :], in0=tmp[k1][:], in1=tmp[k2][:], op=ALU.add),
                          reads=[tmp_b[k1], tmp_b[k2]], writes=[mg_b[d]])
            for f in range(2):
                wob, wob_b = wload(wo_v[:, :, 512 * f:512 * (f + 1)], (KC, 512))
                for dd in range(4):
                    d = 4 * f + dd
                    bk = 4 + dd
                    for c in range(KC):
                        mm(ps[bk][:], wob[:, c, 128 * dd:128 * (dd + 1)], mg[:, c, :], c == 0, c == KC - 1,
                           [wob_b, mg_b[c]], [ps_b[bk]])
                    P.add("dve", lambda e, d=d, bk=bk: e.tensor_tensor(out=xT[:, d, :], in0=ps[bk][:], in1=xT[:, d, :], op=ALU.add),
                          reads=[ps_b[bk], xT_b[d]], writes=[xT_b[d]])
                    norm_feed(d, 2)

        kts_b = [[[Buf(f"kts{l}_{j}_{i}") for i in range(NT)] for j in range(NH)] for l in range(depth)]
        vs_b = [[Buf(f"vs{l}_{i}") for i in range(NT)] for l in range(depth)]
        xm_b = [[[Buf(f"xm{l}_{i}_{c}") for c in range(KC)] for i in range(NT)] for l in range(depth)]

        def x_view(l, i, is_dst):
            t_ = (yT_out if l == depth - 1 else xmid[l]) if is_dst else (xT_in if l == 0 else xmid[l - 1])
            return t_.rearrange("(c p) t -> p c t", p=128)[:, :, i * T:(i + 1) * T]

        def x_load(l, i, c):
            sv = x_view(l, i, False)
            rd = [xm_b[l - 1][i][c]] if l > 0 else []
            P.add("sp", lambda e: e.dma_start(out=xT[:, c, :], in_=sv[:, c, :]), reads=rd, writes=[xT_b[c]], dsem=xT_ds[c])

        def x_store(l, i, c):
            dv = x_view(l, i, True)
            P.add("sp", lambda e: e.dma_start(out=dv[:, c, :], in_=xT[:, c, :]), reads=[xT_b[c]], writes=[xm_b[l][i][c]],
                  dsem=xT_ds[c])

        seq = [(l, i) for l in range(depth) for i in range(NT)]
        for c in range(KC):
            x_load(0, 0, c)
            norm_feed(c, 2)
        for n, (l, i) in enumerate(seq):
            if i == 0:
                layer_setup(l)
            ffn(l, C_F1, "ffn1_w_in", "ffn1_w_out")
            mixer(l, i)

            def post(d, l=l, i=i, n=n):
                x_store(l, i, d)
                if n + 1 < len(seq):
                    x_load(seq[n + 1][0], seq[n + 1][1], d)
                    if d == KC - 1:
                        fence.append(xT_b[d])

            ffn(l, C_F2, "ffn2_w_in", "ffn2_w_out", post=post, feed=(n + 1 < len(seq)))

        if dbg is not None:
            dbg["sbuf_remaining"] = nc.sbuf_bytes_remaining
            dbg["nops"] = {e: len(P.ops[e]) for e in Prog.ENGS}
        block = es.enter_context(nc.Block())
        P.emit(nc, block, esem)
    return nc


def make_cvec(inp, depth):
    cv = np.zeros((128, depth * NCV), np.float32)
    for l in range(depth):
        o = l * NCV

        def pc(v):
            return np.ascontiguousarray(np.asarray(v, np.float32).reshape(KC, 128).T)

        cv[:, o + C_F1:o + C_F1 + 8] = pc(inp["ffn1_norm"][l])
        cv[:, o + C_MIX:o + C_MIX + 8] = pc(inp["mix_norm"][l])
        cv[:, o + C_F2:o + C_F2 + 8] = pc(inp["ffn2_norm"][l])
        dw = np.asarray(inp["conv_dw"][l], np.float32)
        for c in range(KC):
            cv[:, o + C_DW + c * CK:o + C_DW + (c + 1) * CK] = dw[:, c * 128:(c + 1) * 128].T
        cv[:, o + C_DWB:o + C_DWB + 8] = pc(inp["conv_dw_b"][l])
        cv[:, o + C_LNG:o + C_LNG + 8] = pc(inp["conv_ln_g"][l])
        cv[:, o + C_LNB:o + C_LNB + 8] = pc(inp["conv_ln_b"][l])
        cv[:, o + C_QN] = np.tile(np.asarray(inp["q_norm"][l], np.float32), 2)
        cv[:, o + C_KN] = np.tile(np.asarray(inp["k_norm"][l], np.float32), 2)
        cv[:, o + C_SUB] = np.asarray(inp["attn_subln"][l], np.float32)
        cv[:, o + C_LQ:o + C_LQ + 128] = np.asarray(inp["lam_q"][l], np.float32).reshape(1, 128)
        cv[:, o + C_LK:o + C_LK + 128] = np.asarray(inp["lam_k"][l], np.float32).reshape(1, 128)
    return cv


def make_tri():
    k = np.arange(128)[:, None]
    q = np.arange(128)[None, :]
    neg = np.where(q < k, -100.0, 0.0).astype(np.float32)
    return np.concatenate([neg, np.eye(128, dtype=np.float32)], axis=1)


_NC_CACHE = {}


def run(inputs, S, depth, n_cores, trace=False):
    key = (S, depth)
    if key not in _NC_CACHE:
        _NC_CACHE[key] = build_nc(S, depth)
    nc = _NC_CACHE[key]
    x = np.asarray(inputs["x"], np.float32)
    cvec = make_cvec(inputs, depth)
    tri = make_tri()
    wts = {n: np.ascontiguousarray(np.asarray(inputs[n], np.float32)[:depth]) for n in WNAMES}
    in_maps = []
    for b in range(n_cores):
        m = {"xT": np.ascontiguousarray(x[b, :S].T), "cvec": cvec, "tri": tri}
        m.update(wts)
        in_maps.append(m)
    res = run_bass_kernel_spmd(nc, in_maps, core_ids=list(range(n_cores)), trace=trace)
    out = np.stack([np.ascontiguousarray(r["yT"].T) for r in res.results], axis=0)
    return out.astype(np.float32), res


def kernel(**inputs):
    out, _ = run(inputs, 4096, 2, N_CORES)
    return out
```

```python
import numpy as np
from contextlib import ExitStack
import concourse.bass as bass
import concourse.mybir as mybir
from concourse.bass_utils import run_bass_kernel_spmd

F32 = mybir.dt.float32
BF16 = mybir.dt.bfloat16
AF = mybir.ActivationFunctionType
ALU = mybir.AluOpType

D = 1024
KC = 8
DFF = 2816
MC = 22
T = 512
NH = 8
CK = 31
HALO = CK - 1
EPS = 1e-6
IN_COLS = 7168
N_CORES = 8
NSLOT = 6

C_F1 = 0
C_MIX = 8
C_F2 = 16
C_DW = 24
C_DWB = C_DW + 8 * CK
C_LNG = C_DWB + 8
C_LNB = C_LNG + 8
C_QN = C_LNB + 8
C_KN = C_QN + 1
C_SUB = C_KN + 1
C_LQ = C_SUB + 1
C_LK = C_LQ + 128
NCV = C_LK + 128


class Buf:
    __slots__ = ("name", "w", "r")

    def __init__(self, name):
        self.name = name
        self.w = None
        self.r = {}


class DSem:
    __slots__ = ("sem", "count", "last")

    def __init__(self, sem):
        self.sem = sem
        self.count = 0
        self.last = None


class Op:
    __slots__ = ("eng", "fn", "deps", "sig", "sem", "val", "dsem", "ndma")


class Prog:
    ENGS = ("pe", "act", "dve", "pool", "sp")

    def __init__(self):
        self.ops = {e: [] for e in self.ENGS}
        self.dsems = []

    def add(self, eng, fn, reads=(), writes=(), dsem=None, ndma=1):
        op = Op()
        op.eng = eng
        op.fn = fn
        op.dsem = dsem
        op.ndma = ndma
        op.sig = dsem is not None
        op.sem = None
        op.val = 0
        deps = set()
        for b in reads:
            if b.w is not None:
                deps.add(b.w)
        for b in writes:
            if b.w is not None:
                deps.add(b.w)
            for o in b.r.values():
                deps.add(o)
        if dsem is not None and dsem.last is not None:
            deps.add(dsem.last)
        if eng == "pe":
            deps = {d for d in deps if not (d.eng == "pe" and d.dsem is None)}
        op.deps = list(deps)
        for d in op.deps:
            d.sig = True
        rkey = eng if dsem is None else ("dma", id(op))
        for b in reads:
            b.r[rkey] = op
        for b in writes:
            b.w = op
            b.r = {}
        if dsem is not None:
            dsem.last = op
        self.ops[eng].append(op)
        return op

    def emit(self, nc, block, esem):
        for e in self.ENGS:
            cnt = 0
            for op in self.ops[e]:
                if op.dsem is not None:
                    op.dsem.count += 16 * op.ndma
                    op.sem = op.dsem.sem
                    op.val = op.dsem.count
                elif op.sig:
                    cnt += 1
                    op.sem = esem[e]
                    op.val = cnt

        def run(engine, e, final=None):
            known = {}
            for op in self.ops[e]:
                need = {}
                for d in op.deps:
                    k = id(d.sem)
                    if k not in need or need[k][1] < d.val:
                        need[k] = (d.sem, d.val)
                for k, (sem, val) in need.items():
                    if known.get(k, 0) < val:
                        engine.wait_ge(sem, val)
                        known[k] = val
                res = op.fn(engine)
                if op.dsem is not None:
                    if not isinstance(res, (list, tuple)):
                        res = [res]
                    assert len(res) == op.ndma
                    for ins in res:
                        ins.then_inc(op.dsem.sem, 16)
                elif op.sig:
                    res.then_inc(op.sem, 1)
            if final is not None:
                final(engine)

        def final_sp(engine):
            for ds in self.dsems:
                if ds.count > 0:
                    engine.wait_ge(ds.sem, ds.count)

        @block.tensor
        def _(eng):
            run(eng, "pe")

        @block.scalar
        def _(eng):
            run(eng, "act")

        @block.vector
        def _(eng):
            run(eng, "dve")

        @block.gpsimd
        def _(eng):
            run(eng, "pool")

        @block.sync
        def _(eng):
            run(eng, "sp", final_sp)


WNAMES = ["ffn1_w_in", "ffn1_w_out", "w_in", "conv_w_out", "attn_w_out", "w_out", "ffn2_w_in", "ffn2_w_out"]
WSHAPES = {"ffn1_w_in": (D, 2 * DFF), "ffn1_w_out": (DFF, D), "w_in": (D, IN_COLS), "conv_w_out": (D, D),
           "attn_w_out": (D, D), "w_out": (D, D), "ffn2_w_in": (D, 2 * DFF), "ffn2_w_out": (DFF, D)}


def build_nc(S, depth, dbg=None):
    NT = S // T
    nc = bass.Bass("TRN2", target_bir_lowering=False)
    xT_in = nc.dram_tensor("xT", [D, S], F32, kind="ExternalInput").ap()
    cvec_in = nc.dram_tensor("cvec", [128, depth * NCV], F32, kind="ExternalInput").ap()
    tri_in = nc.dram_tensor("tri", [128, 256], F32, kind="ExternalInput").ap()
    W = {}
    for n in WNAMES:
        W[n] = nc.dram_tensor(n, [depth, WSHAPES[n][0], WSHAPES[n][1]], F32, kind="ExternalInput").ap()
    yT_out = nc.dram_tensor("yT", [D, S], F32, kind="ExternalOutput").ap()
    xmid = [nc.dram_tensor(f"xmid{l}", [D, S], F32, kind="Internal").ap() for l in range(max(depth - 1, 1))]
    kts = [nc.dram_tensor(f"kts{l}", [NH, 128, S], BF16, kind="Internal").ap() for l in range(depth)]
    vs = [nc.dram_tensor(f"vs{l}", [NH, S, 128], BF16, kind="Internal").ap() for l in range(depth)]

    P = Prog()
    es = ExitStack()
    with es:
        def sb(name, shape, dt):
            return es.enter_context(nc.sbuf_tensor(name, shape, dt))

        def mksem(name):
            return es.enter_context(nc.semaphore(name))

        def dsem(name):
            d = DSem(mksem(name))
            P.dsems.append(d)
            return d

        esem = {e: mksem("e_" + e) for e in Prog.ENGS}

        xT = sb("xT_sb", [128, KC, T], F32)
        xT_b = [Buf(f"xT{c}") for c in range(KC)]
        xT_ds = [dsem(f"xT_ds{c}") for c in range(KC)]
        hT = sb("hT", [128, KC, T], BF16)
        hT_b = [Buf(f"hT{c}") for c in range(KC)]
        arena = sb("arena", [128, 25 * T], BF16)
        ar_b = [Buf(f"ar{m}") for m in range(25)]
        cT = sb("cT", [128, KC, T], BF16)
        uh = sb("uh", [128, KC, HALO], BF16)
        uh_b = [Buf(f"uh{c}") for c in range(KC)]
        cT_b = [Buf(f"cT{c}") for c in range(KC)]
        qT0 = sb("qT0", [128, NH, T], BF16)
        qT1 = sb("qT1", [128, NH, T], BF16)
        qT_b = [Buf(f"qT{j}") for j in range(NH)]
        aoT = sb("aoT", [128, NH, T], BF16)
        aoT_b = [Buf(f"aoT{j}") for j in range(NH)]
        oc = [sb(f"oc{i}", [128, T], F32) for i in range(2)]
        oc_b = [Buf(f"oc{i}") for i in range(2)]
        rl = [sb(f"rl{i}", [128, T], F32) for i in range(2)]
        rl_b = [Buf(f"rl{i}") for i in range(2)]
        ob = [sb(f"ob{i}", [128, T], F32) for i in range(2)]
        ob_b = [Buf(f"ob{i}") for i in range(2)]
        kn = [sb(f"kn{i}", [128, T], BF16) for i in range(2)]
        kn_b = [Buf(f"kn{i}") for i in range(2)]
        kn_ds = [dsem(f"kn_ds{i}") for i in range(2)]
        vt = [sb(f"vt{i}", [128, D], BF16) for i in range(2)]
        vt_b = [Buf(f"vt{i}") for i in range(2)]
        vt_ds = [dsem(f"vt_ds{i}") for i in range(2)]
        KTj = [sb(f"KTj{i}", [128, S], BF16) for i in range(2)]
        KTj_b = [Buf(f"KTj{i}") for i in range(2)]
        KTj_ds = [dsem(f"KTj_ds{i}") for i in range(2)]
        Vj = [sb(f"Vj{i}", [128, S // 128, 128], BF16) for i in range(2)]
        Vj_b = [Buf(f"Vj{i}") for i in range(2)]
        Vj_ds = [dsem(f"Vj_ds{i}") for i in range(2)]
        NPT = 3
        pt = [sb(f"pt{i}", [128, T], BF16) for i in range(NPT)]
        pt_b = [Buf(f"pt{i}") for i in range(NPT)]
        NTMP = 6
        tmp = [sb(f"tmp{i}", [128, T], F32) for i in range(NTMP)]
        tmp_b = [Buf(f"tmp{i}") for i in range(NTMP)]
        sqb = [sb(f"sqb{i}", [128, T], BF16) for i in range(3)]
        sqb_b = [Buf(f"sqb{i}") for i in range(3)]
        NTB = 3
        tmpb = [sb(f"tmpb{i}", [128, T], BF16) for i in range(NTB)]
        tmpb_b = [Buf(f"tmpb{i}") for i in range(NTB)]
        wslot = [sb(f"wslot{i}", [128, KC * T], BF16) for i in range(NSLOT)]
        wslot_b = [Buf(f"wslot{i}") for i in range(NSLOT)]
        wslot_ds = [dsem(f"wslot_ds{i}") for i in range(NSLOT)]
        cv = sb("cv", [128, depth * NCV], F32)
        cv_b = Buf("cv")
        cv_ds = dsem("cv_ds")
        tri = sb("tri_sb", [128, 256], BF16)
        tri_b = Buf("tri")
        tri_ds = dsem("tri_ds")
        ones_d = sb("ones_d", [128, 128], BF16)
        ones_h = sb("ones_h", [128, 128], BF16)
        ones_1 = sb("ones_1", [128, 128], BF16)
        ones_q = sb("ones_q", [128, 128], BF16)
        const_b = Buf("consts")
        epsc = sb("epsc", [128, 1], F32)
        lay = sb("lay", [128, depth * 8], F32)
        lay_b = Buf("lay")
        lamt = sb("lamt", [128, 128], F32)
        lamt_b = Buf("lamt")
        lam2 = sb("lam2", [128, 4], F32)
        lam2_b = Buf("lam2")

        ps = [es.enter_context(nc.psum_tensor(f"ps{i}", [128, T], F32)) for i in range(8)]
        ps_b = [Buf(f"ps{i}") for i in range(8)]

        class Rot:
            def __init__(self, n):
                self.n = n
                self.i = 0

            def next(self):
                k = self.i % self.n
                self.i += 1
                return k

        tmp_rot = Rot(NTMP)
        tmpb_rot = Rot(NTB)
        slot_rot = Rot(NSLOT)

        P.add("sp", lambda e: e.dma_start(out=cv[:], in_=cvec_in), writes=[cv_b], dsem=cv_ds)
        P.add("pool", lambda e: e.dma_start(out=tri[:], in_=tri_in), writes=[tri_b], dsem=tri_ds)
        P.add("dve", lambda e: e.memset(ones_d[:], 1.0 / D), writes=[const_b])
        P.add("dve", lambda e: e.memset(ones_h[:], 1.0 / 128), writes=[const_b])
        P.add("dve", lambda e: e.memset(ones_1[:], 1.0), writes=[const_b])
        P.add("dve", lambda e: e.memset(ones_q[:], 0.0), writes=[const_b])
        P.add("dve", lambda e: e.memset(ones_q[0:64, 0:64], 1.0 / 64), writes=[const_b])
        P.add("dve", lambda e: e.memset(ones_q[64:128, 64:128], 1.0 / 64), writes=[const_b])
        P.add("dve", lambda e: e.memset(epsc[:], EPS), writes=[const_b])
        P.add("dve", lambda e: e.memset(qT0[:], 0.0), writes=qT_b)
        P.add("dve", lambda e: e.memset(qT1[:], 0.0), writes=qT_b)

        def cvcol(l, col, n=1):
            return cv[:, l * NCV + col: l * NCV + col + n]

        fence = []

        def wload(src_ap, shape):
            k = slot_rot.next()
            a, b = shape
            dst = wslot[k][:, 0:a * b].rearrange("p (a b) -> p a b", a=a)
            rd = list(fence)
            del fence[:]
            P.add("pool", lambda e, dst=dst, src=src_ap: e.dma_start(out=dst, in_=src),
                  reads=rd, writes=[wslot_b[k]], dsem=wslot_ds[k])
            return dst, wslot_b[k]

        def wview(name, l):
            return W[name][l].rearrange("(kc p) n -> p kc n", p=128)

        def mm(out, lhsT, rhs, start, stop, reads, writes):
            P.add("pe", lambda e: e.matmul(out, lhsT=lhsT, rhs=rhs, start=start, stop=stop), reads=reads, writes=writes)

        def rstd_from_mean(ms_ps, ms_buf, n=T):
            k = tmp_rot.next()
            P.add("act", lambda e: e.activation(out=tmp[k][:, 0:n], in_=ms_ps, func=AF.Ln, bias=epsc[:, 0:1], scale=1.0),
                  reads=[ms_buf, const_b], writes=[tmp_b[k]])
            P.add("act", lambda e: e.activation(out=tmp[k][:, 0:n], in_=tmp[k][:, 0:n], func=AF.Exp, scale=-0.5),
                  reads=[tmp_b[k]], writes=[tmp_b[k]])
            return k

        NBANK = 0
        sq_rot = Rot(3)
        norm_pending = []

        def norm_feed(c, lag):
            kb_ = sq_rot.next()
            P.add("act", lambda e: e.activation(out=sqb[kb_][:], in_=xT[:, c, :], func=AF.Square),
                  reads=[xT_b[c]], writes=[sqb_b[kb_]])

            def stats():
                mm(ps[NBANK][:], ones_d[:], sqb[kb_][:], c == 0, c == KC - 1, [sqb_b[kb_], const_b], [ps_b[NBANK]])
            norm_pending.append(stats)
            while len(norm_pending) > lag:
                norm_pending.pop(0)()

        def norm_flush():
            while norm_pending:
                norm_pending.pop(0)()

        def norm_finish(l, gcol):
            norm_flush()
            kr = rstd_from_mean(ps[NBANK][:], ps_b[NBANK])
            for c in range(KC):
                P.add("dve", lambda e, c=c: e.scalar_tensor_tensor(out=hT[:, c, :], in0=xT[:, c, :],
                                                                  scalar=cvcol(l, gcol + c), in1=tmp[kr][:],
                                                                  op0=ALU.mult, op1=ALU.mult),
                      reads=[xT_b[c], tmp_b[kr], cv_b], writes=[hT_b[c]])

        hid = arena[:, 0:MC * T].rearrange("p (m t) -> p m t", m=MC)
        mg = arena[:, 0:KC * T].rearrange("p (m t) -> p m t", m=KC)
        mg_b = ar_b[0:KC]

        def ffn(l, gcol, w_in_name, w_out_name, post=None, feed=True):
            norm_finish(l, gcol)
            wv = wview(w_in_name, l)
            wo = W[w_out_name][l].rearrange("(m p) n -> p m n", p=128)
            ngrp = (MC + 3) // 4
            for g in range(ngrp):
                nm = min(4, MC - 4 * g)
                wa, wa_b = wload(wv[:, :, 512 * g: 512 * g + 128 * nm], (KC, 128 * nm))
                wb, wb_b = wload(wv[:, :, DFF + 512 * g: DFF + 512 * g + 128 * nm], (KC, 128 * nm))
                for mi in range(nm):
                    m = 4 * g + mi
                    ba = (2 * m) % 4
                    bb = (2 * m + 1) % 4
                    for c in range(KC):
                        mm(ps[ba][:], wa[:, c, 128 * mi:128 * (mi + 1)], hT[:, c, :], c == 0, c == KC - 1,
                           [wa_b, hT_b[c]], [ps_b[ba]])
                    for c in range(KC):
                        mm(ps[bb][:], wb[:, c, 128 * mi:128 * (mi + 1)], hT[:, c, :], c == 0, c == KC - 1,
                           [wb_b, hT_b[c]], [ps_b[bb]])
                    k = tmp_rot.next()
                    P.add("act", lambda e, k=k, ba=ba: e.activation(out=tmp[k][:], in_=ps[ba][:], func=AF.Silu),
                          reads=[ps_b[ba]], writes=[tmp_b[k]])
                    P.add("dve", lambda e, k=k, bb=bb, m=m: e.tensor_tensor(out=hid[:, m, :], in0=ps[bb][:], in1=tmp[k][:],
                                                                           op=ALU.mult),
                          reads=[ps_b[bb], tmp_b[k]], writes=[ar_b[m]])
            for d in range(KC):
                wd, wd_b = wload(wo[:, :, 128 * d:128 * (d + 1)], (MC, 128))
                bk = 4 + (d % 4)
                for m in range(MC):
                    mm(ps[bk][:], wd[:, m, :], hid[:, m, :], m == 0, m == MC - 1, [wd_b, ar_b[m]], [ps_b[bk]])
                P.add("dve", lambda e, d=d, bk=bk: e.scalar_tensor_tensor(out=xT[:, d, :], in0=ps[bk][:], scalar=0.5,
                                                                         in1=xT[:, d, :], op0=ALU.mult, op1=ALU.add),
                      reads=[ps_b[bk], xT_b[d]], writes=[xT_b[d]])
                if post is not None:
                    post(d)
                if feed:
                    norm_feed(d, 2)

        UW = 544
        u_all = arena[:, 0:KC * UW].rearrange("p (c w) -> p c w", c=KC)
        Y0 = 9 * T
        y_all = arena[:, Y0:Y0 + 2 * KC * T].bitcast(F32).rearrange("p (c t) -> p c t", c=KC)

        def u_bufs(c):
            lo = (c * UW * 2) // 1024
            hi = ((c + 1) * UW * 2 - 1) // 1024
            return [ar_b[i] for i in range(lo, hi + 1)]

        def y_bufs(c):
            return [ar_b[9 + 2 * c], ar_b[10 + 2 * c]]

        def layer_setup(l):
            lam_init = 0.8 - 0.6 * float(np.exp(-0.3 * l))
            P.add("dve", lambda e: e.tensor_single_scalar(out=lay[:, l * 8 + 0:l * 8 + 1], in_=cvcol(l, C_QN), scalar=0.125, op=ALU.mult),
                  reads=[cv_b], writes=[lay_b])
            P.add("dve", lambda e: e.tensor_single_scalar(out=lay[:, l * 8 + 1:l * 8 + 2], in_=cvcol(l, C_SUB), scalar=1.0 - lam_init, op=ALU.mult),
                  reads=[cv_b], writes=[lay_b])
            P.add("dve", lambda e: e.tensor_tensor(out=lamt[:], in0=cvcol(l, C_LQ, 128), in1=cvcol(l, C_LK, 128), op=ALU.mult),
                  reads=[cv_b], writes=[lamt_b])
            P.add("dve", lambda e: e.reduce_sum(out=lam2[:, 0:2], in_=lamt[:].rearrange("p (a b) -> p a b", a=2),
                                                axis=mybir.AxisListType.X),
                  reads=[lamt_b], writes=[lam2_b])
            P.add("act", lambda e: e.activation(out=lam2[:, 2:4], in_=lam2[:, 0:2], func=AF.Exp), reads=[lam2_b], writes=[lam2_b])
            P.add("dve", lambda e: e.scalar_tensor_tensor(out=lay[:, l * 8 + 2:l * 8 + 3], in0=lam2[:, 3:4], scalar=-lam_init,
                                                         in1=lam2[:, 2:3], op0=ALU.add, op1=ALU.subtract),
                  reads=[lam2_b], writes=[lay_b])

        def mixer(l, i):
            wv = wview("w_in", l)
            norm_finish(l, C_MIX)
            tok0 = i * T
            wvb = [wload(wv[:, :, 4096 + 512 * f: 4096 + 512 * (f + 1)], (KC, 512)) for f in range(2)]
            for tc in range(T // 128):
                r = tc % 2
                for f in range(2):
                    bk = (2 * tc + f) % 4
                    for c in range(KC):
                        mm(ps[bk][:], hT[:, c, 128 * tc:128 * (tc + 1)], wvb[f][0][:, c, :], c == 0, c == KC - 1,
                           [hT_b[c], wvb[f][1]], [ps_b[bk]])
                    P.add("act", lambda e, r=r, f=f, bk=bk: e.activation(out=vt[r][:, 512 * f:512 * (f + 1)], in_=ps[bk][:], func=AF.Copy),
                          reads=[ps_b[bk]], writes=[vt_b[r]])
                dst = vs[l][:, tok0 + 128 * tc: tok0 + 128 * (tc + 1), :].rearrange("h t e -> t h e")
                src = vt[r][:].rearrange("p (h e) -> p h e", h=NH)
                P.add("sp", lambda e, dst=dst, src=src: e.dma_start(out=dst, in_=src), reads=[vt_b[r]],
                      writes=[vs_b[l][i]], dsem=vt_ds[r])
            pending = None
            for which in ("k", "q"):
                base = 3072 if which == "k" else 2048
                blocks = [wload(wv[:, :, base + 512 * f: base + 512 * (f + 1)], (KC, 512)) for f in range(2)]
                for j in range(NH):
                    wblk, wblk_b = blocks[j // 4]
                    bk = j % 4
                    for c in range(KC):
                        mm(ps[bk][:], wblk[:, c, 128 * (j % 4):128 * (j % 4 + 1)], hT[:, c, :], c == 0, c == KC - 1,
                           [wblk_b, hT_b[c]], [ps_b[bk]])
                    ksq = tmpb_rot.next()
                    P.add("act", lambda e, ksq=ksq, bk=bk: e.activation(out=tmpb[ksq][:], in_=ps[bk][:], func=AF.Square),
                          reads=[ps_b[bk]], writes=[tmpb_b[ksq]])

                    def tail(which=which, j=j, bk=bk, ksq=ksq):
                        b2 = 4 + (j % 2)
                        mm(ps[b2][:], ones_q[:], tmpb[ksq][:], True, True, [tmpb_b[ksq], const_b], [ps_b[b2]])
                        kr = rstd_from_mean(ps[b2][:], ps_b[b2])
                        if which == "k":
                            r = j % 2
                            P.add("dve", lambda e: e.scalar_tensor_tensor(
                                out=kn[r][:], in0=ps[bk][:], scalar=cvcol(l, C_KN), in1=tmp[kr][:], op0=ALU.mult, op1=ALU.mult),
                                reads=[ps_b[bk], tmp_b[kr], cv_b], writes=[kn_b[r]])
                            P.add("sp", lambda e: e.dma_start(out=kts[l][j, :, tok0:tok0 + T], in_=kn[r][:]),
                                  reads=[kn_b[r]], writes=[kts_b[l][j][i]], dsem=kn_ds[r])
                        else:
                            gq = lay[:, l * 8:l * 8 + 1]
                            P.add("dve", lambda e: e.scalar_tensor_tensor(
                                out=qT0[0:64, j, :], in0=ps[bk][0:64, :], scalar=gq[0:64, :], in1=tmp[kr][0:64, :],
                                op0=ALU.mult, op1=ALU.mult),
                                reads=[ps_b[bk], tmp_b[kr], lay_b], writes=[qT_b[j]])
                            P.add("dve", lambda e: e.scalar_tensor_tensor(
                                out=qT1[64:128, j, :], in0=ps[bk][64:128, :], scalar=gq[64:128, :], in1=tmp[kr][64:128, :],
                                op0=ALU.mult, op1=ALU.mult),
                                reads=[ps_b[bk], tmp_b[kr], lay_b], writes=[qT_b[j]])
                    if pending is not None:
                        pending()
                    pending = tail
            if pending is not None:
                pending()
                pending = None
            if i == 0:
                P.add("dve", lambda e: e.memset(u_all[:, :, 0:HALO], 0.0), writes=[ar_b[k_] for k_ in range(9)])
            else:
                for cc in range(KC):
                    P.add("act", lambda e, cc=cc: e.activation(out=u_all[:, cc, 0:HALO], in_=uh[:, cc, :], func=AF.Copy),
                          reads=[uh_b[cc]], writes=u_bufs(cc))
            ablk = [wload(wv[:, :, 512 * f: 512 * (f + 1)], (KC, 512)) for f in range(2)]
            gblk = [wload(wv[:, :, 1024 + 512 * f: 1024 + 512 * (f + 1)], (KC, 512)) for f in range(2)]
            for cc in range(KC):
                ba = (2 * cc) % 4
                bg = (2 * cc + 1) % 4
                wa, wa_b = ablk[cc // 4]
                wg, wg_b = gblk[cc // 4]
                for c in range(KC):
                    mm(ps[bg][:], wg[:, c, 128 * (cc % 4):128 * (cc % 4 + 1)], hT[:, c, :], c == 0, c == KC - 1,
                       [wg_b, hT_b[c]], [ps_b[bg]])
                for c in range(KC):
                    mm(ps[ba][:], wa[:, c, 128 * (cc % 4):128 * (cc % 4 + 1)], hT[:, c, :], c == 0, c == KC - 1,
                       [wa_b, hT_b[c]], [ps_b[ba]])
                k = tmp_rot.next()
                P.add("act", lambda e, k=k, bg=bg: e.activation(out=tmp[k][:], in_=ps[bg][:], func=AF.Sigmoid),
                      reads=[ps_b[bg]], writes=[tmp_b[k]])
                P.add("dve", lambda e, k=k, ba=ba, cc=cc: e.tensor_tensor(out=u_all[:, cc, HALO:HALO + T], in0=ps[ba][:], in1=tmp[k][:],
                                                                         op=ALU.mult),
                      reads=[ps_b[ba], tmp_b[k]], writes=u_bufs(cc))

            nkb = 4 * (i + 1)
            L = nkb * 128
            cg = conv_gen(l)
            NCONV_HEADS = 7
            quota = (KC * CK + NCONV_HEADS - 1) // NCONV_HEADS

            def pump(k):
                for _ in range(k):
                    if next(cg, "done") == "done":
                        return

            carry = []
            for j in range(NH):
                fin2 = attention_head(l, i, j, nkb, L, carry, pump if j < NCONV_HEADS else None, quota)
                carry = [(5, fin2)]
                if j == NCONV_HEADS - 1:
                    pump(KC * CK + 8)
                    carry.append((11, lambda sbk: conv_ln(l)))
            while carry:
                carry.pop(0)[1](0)
            merge_and_out(l)

        def attention_head(l, i, j, nkb, L, carry, pump, quota):
            s = j % 2
            P.add("sp", lambda e: e.dma_start(out=KTj[s][:, 0:L], in_=kts[l][j, :, 0:L]),
                  reads=[kts_b[l][j][t_] for t_ in range(i + 1)], writes=[KTj_b[s]], dsem=KTj_ds[s])
            P.add("sp", lambda e: e.dma_start(out=Vj[s][:, 0:nkb, :], in_=vs[l][j, 0:L, :].rearrange("(n p) e -> p n e", p=128)),
                  reads=[vs_b[l][t_] for t_ in range(i + 1)], writes=[Vj_b[s]], dsem=Vj_ds[s])
            blocks = [(kb, c) for kb in range(nkb) for c in range(2)]
            OB = [4, 5]
            LB = [6, 7]

            def s_mm(n):
                kb, c = blocks[n]
                r = kb - 4 * i
                c0 = 128 * r if r > 0 else 0
                sbk = n % 4
                q = (qT0 if c == 0 else qT1)
                mm(ps[sbk][:, c0:T], KTj[s][:, 128 * kb:128 * (kb + 1)], q[:, j, c0:T], True, r < 0,
                   [KTj_b[s], qT_b[j]], [ps_b[sbk]])
                if r >= 0:
                    mm(ps[sbk][:, c0:c0 + 128], tri[:, 128:256], tri[:, 0:128], False, True, [tri_b], [ps_b[sbk]])
                p_ = n % NPT
                P.add("act", lambda e: e.activation(out=pt[p_][:, c0:T], in_=ps[sbk][:, c0:T], func=AF.Exp),
                      reads=[ps_b[sbk]], writes=[pt_b[p_]])

            def pv_mm(n):
                kb, c = blocks[n]
                r = kb - 4 * i
                c0 = 128 * r if r > 0 else 0
                p_ = n % NPT
                mm(ps[OB[c]][:, c0:T], Vj[s][:, kb, :], pt[p_][:, c0:T], kb == 0, kb == nkb - 1,
                   [Vj_b[s], pt_b[p_]], [ps_b[OB[c]]])
                mm(ps[LB[c]][:, c0:T], ones_1[:], pt[p_][:, c0:T], kb == 0, kb == nkb - 1,
                   [const_b, pt_b[p_]], [ps_b[LB[c]]])

            nb = len(blocks)
            s_mm(0)
            if nb > 1:
                s_mm(1)
            for n in range(nb):
                if n + 2 < nb:
                    s_mm(n + 2)
                pv_mm(n)
                if pump is not None:
                    k_ = (quota * (n + 1)) // nb - (quota * n) // nb
                    pump(k_)
                while carry and n >= carry[0][0]:
                    carry.pop(0)[1]((n + 3) % 4)
            while carry:
                carry.pop(0)[1](0)
            for c in range(2):
                P.add("act", lambda e, c=c: e.activation(out=rl[c][:], in_=ps[LB[c]][:], func=AF.Ln),
                      reads=[ps_b[LB[c]]], writes=[rl_b[c]])
                P.add("dve", lambda e, c=c: e.tensor_copy(out=oc[c][:], in_=ps[OB[c]][:]),
                      reads=[ps_b[OB[c]]], writes=[oc_b[c]])
            ko = j % 2

            def fin2(sbk):
                a = []
                for c in range(2):
                    P.add("act", lambda e, c=c: e.activation(out=rl[c][:], in_=rl[c][:], func=AF.Exp, scale=-1.0),
                          reads=[rl_b[c]], writes=[rl_b[c]])
                    ka = tmp_rot.next()
                    P.add("dve", lambda e, ka=ka, c=c: e.tensor_tensor(out=tmp[ka][:], in0=oc[c][:], in1=rl[c][:], op=ALU.mult),
                          reads=[oc_b[c], rl_b[c]], writes=[tmp_b[ka]])
                    a.append(ka)
                P.add("dve", lambda e: e.scalar_tensor_tensor(out=ob[ko][:], in0=tmp[a[1]][:], scalar=lay[:, l * 8 + 2:l * 8 + 3],
                                                             in1=tmp[a[0]][:], op0=ALU.mult, op1=ALU.add),
                      reads=[tmp_b[a[0]], tmp_b[a[1]], lay_b], writes=[ob_b[ko]])
                ksq = tmpb_rot.next()
                P.add("act", lambda e: e.activation(out=tmpb[ksq][:], in_=ob[ko][:], func=AF.Square),
                      reads=[ob_b[ko]], writes=[tmpb_b[ksq]])
                mm(ps[sbk][:], ones_h[:], tmpb[ksq][:], True, True, [tmpb_b[ksq], const_b], [ps_b[sbk]])
                kr = rstd_from_mean(ps[sbk][:], ps_b[sbk])
                P.add("dve", lambda e: e.scalar_tensor_tensor(out=aoT[:, j, :], in0=ob[ko][:], scalar=lay[:, l * 8 + 1:l * 8 + 2],
                                                             in1=tmp[kr][:], op0=ALU.mult, op1=ALU.mult),
                      reads=[ob_b[ko], tmp_b[kr], lay_b], writes=[aoT_b[j]])
            return fin2

        def conv_gen(l):
            for c0 in range(0, KC, 2):
                for k in range(CK):
                    for c in (c0, c0 + 1):
                        yb = y_bufs(c)
                        ub = u_bufs(c)
                        if k == 0:
                            P.add("dve", lambda e, c=c: e.tensor_scalar(out=y_all[:, c, :], in0=u_all[:, c, 0:T],
                                                                       scalar1=cvcol(l, C_DW + c * CK), scalar2=cvcol(l, C_DWB + c),
                                                                       op0=ALU.mult, op1=ALU.add),
                                  reads=ub + [cv_b], writes=yb)
                        else:
                            P.add("dve", lambda e, c=c, k=k: e.scalar_tensor_tensor(out=y_all[:, c, :], in0=u_all[:, c, k:k + T],
                                                                              scalar=cvcol(l, C_DW + c * CK + k), in1=y_all[:, c, :],
                                                                              op0=ALU.mult, op1=ALU.add),
                                  reads=ub + yb + [cv_b], writes=yb)
                        yield
                for c in (c0, c0 + 1):
                    P.add("act", lambda e, c=c: e.activation(out=uh[:, c, :], in_=u_all[:, c, T:T + HALO], func=AF.Copy),
                          reads=u_bufs(c), writes=[uh_b[c]])
                for c in (c0, c0 + 1):
                    P.add("act", lambda e, c=c: e.activation(out=u_all[:, c, 0:T], in_=y_all[:, c, :], func=AF.Copy),
                          reads=y_bufs(c) + u_bufs(c), writes=u_bufs(c))
                    P.add("act", lambda e, c=c: e.activation(out=cT[:, c, :], in_=y_all[:, c, :], func=AF.Square),
                          reads=y_bufs(c), writes=[cT_b[c]])

        def conv_ln(l):
            for c in range(KC):
                mm(ps[0][:], ones_d[:], u_all[:, c, 0:T], c == 0, c == KC - 1, u_bufs(c) + [const_b], [ps_b[0]])
                mm(ps[1][:], ones_d[:], cT[:, c, :], c == 0, c == KC - 1, [cT_b[c], const_b], [ps_b[1]])
            kmean = tmp_rot.next()
            kvar = tmp_rot.next()
            kms = tmp_rot.next()
            P.add("act", lambda e: e.activation(out=tmp[kmean][:], in_=ps[0][:], func=AF.Copy), reads=[ps_b[0]], writes=[tmp_b[kmean]])
            P.add("act", lambda e: e.activation(out=tmp[kms][:], in_=ps[1][:], func=AF.Copy), reads=[ps_b[1]], writes=[tmp_b[kms]])
            P.add("dve", lambda e: e.tensor_tensor(out=tmp[kvar][:], in0=tmp[kmean][:], in1=tmp[kmean][:], op=ALU.mult),
                  reads=[tmp_b[kmean]], writes=[tmp_b[kvar]])
            P.add("dve", lambda e: e.tensor_tensor(out=tmp[kvar][:], in0=tmp[kms][:], in1=tmp[kvar][:], op=ALU.subtract),
                  reads=[tmp_b[kms], tmp_b[kvar]], writes=[tmp_b[kvar]])
            P.add("dve", lambda e: e.tensor_single_scalar(out=tmp[kvar][:], in_=tmp[kvar][:], scalar=0.0, op=ALU.max),
                  reads=[tmp_b[kvar]], writes=[tmp_b[kvar]])
            kr = rstd_from_mean(tmp[kvar][:], tmp_b[kvar])
            knm = tmp_rot.next()
            P.add("dve", lambda e: e.scalar_tensor_tensor(out=tmp[knm][:], in0=tmp[kmean][:], scalar=-1.0, in1=tmp[kr][:],
                                                         op0=ALU.mult, op1=ALU.mult),
                  reads=[tmp_b[kmean], tmp_b[kr]], writes=[tmp_b[knm]])
            for c in range(KC):
                P.add("dve", lambda e, c=c: e.tensor_tensor(out=y_all[:, c, :], in0=y_all[:, c, :], in1=tmp[kr][:], op=ALU.mult),
                      reads=y_bufs(c) + [tmp_b[kr]], writes=y_bufs(c))
                P.add("dve", lambda e, c=c: e.tensor_tensor(out=y_all[:, c, :], in0=y_all[:, c, :], in1=tmp[knm][:], op=ALU.add),
                      reads=y_bufs(c) + [tmp_b[knm]], writes=y_bufs(c))
                P.add("act", lambda e, c=c: e.activation(out=cT[:, c, :], in_=y_all[:, c, :], func=AF.Silu,
                                                        bias=cvcol(l, C_LNB + c), scale=cvcol(l, C_LNG + c)),
                      reads=y_bufs(c) + [cv_b], writes=[cT_b[c]])

        def merge_and_out(l):
            wv = wview("w_in", l)
            wc_v = wview("conv_w_out", l)
            wa_v = wview("attn_w_out", l)
            wo_v = wview("w_out", l)
            for f in range(2):
                gcb, gcb_b = wload(wv[:, :, 5120 + 512 * f: 5120 + 512 * (f + 1)], (KC, 512))
                wcb, wcb_b = wload(wc_v[:, :, 512 * f:512 * (f + 1)], (KC, 512))
                gab, gab_b = wload(wv[:, :, 6144 + 512 * f: 6144 + 512 * (f + 1)], (KC, 512))
                wab, wab_b = wload(wa_v[:, :, 512 * f:512 * (f + 1)], (KC, 512))
                for dd in range(4):
                    d = 4 * f + dd
                    cs = slice(128 * dd, 128 * (dd + 1))
                    for c in range(KC):
                        mm(ps[0][:], gcb[:, c, cs], hT[:, c, :], c == 0, c == KC - 1, [gcb_b, hT_b[c]], [ps_b[0]])
                    for c in range(KC):
                        mm(ps[1][:], wcb[:, c, cs], cT[:, c, :], c == 0, c == KC - 1, [wcb_b, cT_b[c]], [ps_b[1]])
                    for c in range(KC):
                        mm(ps[2][:], gab[:, c, cs], hT[:, c, :], c == 0, c == KC - 1, [gab_b, hT_b[c]], [ps_b[2]])
                    for c in range(KC):
                        mm(ps[3][:], wab[:, c, cs], aoT[:, c, :], c == 0, c == KC - 1, [wab_b, aoT_b[c]], [ps_b[3]])
                    k1 = tmp_rot.next()
                    k2 = tmp_rot.next()
                    P.add("act", lambda e, k1=k1: e.activation(out=tmp[k1][:], in_=ps[0][:], func=AF.Sigmoid),
                          reads=[ps_b[0]], writes=[tmp_b[k1]])
                    P.add("act", lambda e, k2=k2: e.activation(out=tmp[k2][:], in_=ps[2][:], func=AF.Sigmoid),
                          reads=[ps_b[2]], writes=[tmp_b[k2]])
                    P.add("dve", lambda e, k1=k1: e.tensor_tensor(out=tmp[k1][:], in0=ps[1][:], in1=tmp[k1][:], op=ALU.mult),
                          reads=[ps_b[1], tmp_b[k1]], writes=[tmp_b[k1]])
                    P.add("dve", lambda e, k2=k2: e.tensor_tensor(out=tmp[k2][:], in0=ps[3][:], in1=tmp[k2][:], op=ALU.mult),
                          reads=[ps_b[3], tmp_b[k2]], writes=[tmp_b[k2]])
                    P.add("dve", lambda e, k1=k1, k2=k2, d=d: e.tensor_tensor(out=mg[:, d, :], in0=tmp[k1][:], in1=tmp[k2][:], op=ALU.add),
                          reads=[tmp_b[k1], tmp_b[k2]], writes=[mg_b[d]])
            for f in range(2):
                wob, wob_b = wload(wo_v[:, :, 512 * f:512 * (f + 1)], (KC, 512))
                for dd in range(4):
                    d = 4 * f + dd
                    bk = 4 + dd
                    for c in range(KC):
                        mm(ps[bk][:], wob[:, c, 128 * dd:128 * (dd + 1)], mg[:, c, :], c == 0, c == KC - 1,
                           [wob_b, mg_b[c]], [ps_b[bk]])
                    P.add("dve", lambda e, d=d, bk=bk: e.tensor_tensor(out=xT[:, d, :], in0=ps[bk][:], in1=xT[:, d, :], op=ALU.add),
                          reads=[ps_b[bk], xT_b[d]], writes=[xT_b[d]])
                    norm_feed(d, 2)

        kts_b = [[[Buf(f"kts{l}_{j}_{i}") for i in range(NT)] for j in range(NH)] for l in range(depth)]
        vs_b = [[Buf(f"vs{l}_{i}") for i in range(NT)] for l in range(depth)]
        xm_b = [[[Buf(f"xm{l}_{i}_{c}") for c in range(KC)] for i in range(NT)] for l in range(depth)]

        def x_view(l, i, is_dst):
            t_ = (yT_out if l == depth - 1 else xmid[l]) if is_dst else (xT_in if l == 0 else xmid[l - 1])
            return t_.rearrange("(c p) t -> p c t", p=128)[:, :, i * T:(i + 1) * T]

        def x_load(l, i, c):
            sv = x_view(l, i, False)
            rd = [xm_b[l - 1][i][c]] if l > 0 else []
            P.add("sp", lambda e: e.dma_start(out=xT[:, c, :], in_=sv[:, c, :]), reads=rd, writes=[xT_b[c]], dsem=xT_ds[c])

        def x_store(l, i, c):
            dv = x_view(l, i, True)
            P.add("sp", lambda e: e.dma_start(out=dv[:, c, :], in_=xT[:, c, :]), reads=[xT_b[c]], writes=[xm_b[l][i][c]],
                  dsem=xT_ds[c])

        seq = [(l, i) for l in range(depth) for i in range(NT)]
        for c in range(KC):
            x_load(0, 0, c)
            norm_feed(c, 2)
        for n, (l, i) in enumerate(seq):
            if i == 0:
                layer_setup(l)
            ffn(l, C_F1, "ffn1_w_in", "ffn1_w_out")
            mixer(l, i)

            def post(d, l=l, i=i, n=n):
                x_store(l, i, d)
                if n + 1 < len(seq):
                    x_load(seq[n + 1][0], seq[n + 1][1], d)
                    if d == KC - 1:
                        fence.append(xT_b[d])

            ffn(l, C_F2, "ffn2_w_in", "ffn2_w_out", post=post, feed=(n + 1 < len(seq)))

        if dbg is not None:
            dbg["sbuf_remaining"] = nc.sbuf_bytes_remaining
            dbg["nops"] = {e: len(P.ops[e]) for e in Prog.ENGS}
        block = es.enter_context(nc.Block())
        P.emit(nc, block, esem)
    return nc


def make_cvec(inp, depth):
    cv = np.zeros((128, depth * NCV), np.float32)
    for l in range(depth):
        o = l * NCV

        def pc(v):
            return np.ascontiguousarray(np.asarray(v, np.float32).reshape(KC, 128).T)

        cv[:, o + C_F1:o + C_F1 + 8] = pc(inp["ffn1_norm"][l])
        cv[:, o + C_MIX:o + C_MIX + 8] = pc(inp["mix_norm"][l])
        cv[:, o + C_F2:o + C_F2 + 8] = pc(inp["ffn2_norm"][l])
        dw = np.asarray(inp["conv_dw"][l], np.float32)
        for c in range(KC):
            cv[:, o + C_DW + c * CK:o + C_DW + (c + 1) * CK] = dw[:, c * 128:(c + 1) * 128].T
        cv[:, o + C_DWB:o + C_DWB + 8] = pc(inp["conv_dw_b"][l])
        cv[:, o + C_LNG:o + C_LNG + 8] = pc(inp["conv_ln_g"][l])
        cv[:, o + C_LNB:o + C_LNB + 8] = pc(inp["conv_ln_b"][l])
        cv[:, o + C_QN] = np.tile(np.asarray(inp["q_norm"][l], np.float32), 2)
        cv[:, o + C_KN] = np.tile(np.asarray(inp["k_norm"][l], np.float32), 2)
        cv[:, o + C_SUB] = np.asarray(inp["attn_subln"][l], np.float32)
        cv[:, o + C_LQ:o + C_LQ + 128] = np.asarray(inp["lam_q"][l], np.float32).reshape(1, 128)
        cv[:, o + C_LK:o + C_LK + 128] = np.asarray(inp["lam_k"][l], np.float32).reshape(1, 128)
    return cv


def make_tri():
    k = np.arange(128)[:, None]
    q = np.arange(128)[None, :]
    neg = np.where(q < k, -100.0, 0.0).astype(np.float32)
    return np.concatenate([neg, np.eye(128, dtype=np.float32)], axis=1)


_NC_CACHE = {}


def run(inputs, S, depth, n_cores, trace=False):
    key = (S, depth)
    if key not in _NC_CACHE:
        _NC_CACHE[key] = build_nc(S, depth)
    nc = _NC_CACHE[key]
    x = np.asarray(inputs["x"], np.float32)
    cvec = make_cvec(inputs, depth)
    tri = make_tri()
    wts = {n: np.ascontiguousarray(np.asarray(inputs[n], np.float32)[:depth]) for n in WNAMES}
    in_maps = []
    for b in range(n_cores):
        m = {"xT": np.ascontiguousarray(x[b, :S].T), "cvec": cvec, "tri": tri}
        m.update(wts)
        in_maps.append(m)
    res = run_bass_kernel_spmd(nc, in_maps, core_ids=list(range(n_cores)), trace=trace)
    out = np.stack([np.ascontiguousarray(r["yT"].T) for r in res.results], axis=0)
    return out.astype(np.float32), res


def kernel(**inputs):
    out, _ = run(inputs, 4096, 2, N_CORES)
    return out
```

```python
import numpy as np
from contextlib import ExitStack
import concourse.bass as bass
import concourse.mybir as mybir
from concourse.bass_utils import run_bass_kernel_spmd

F32 = mybir.dt.float32
BF16 = mybir.dt.bfloat16
AF = mybir.ActivationFunctionType
ALU = mybir.AluOpType

D = 1024
KC = 8
DFF = 2816
MC = 22
T = 512
NH = 8
CK = 31
HALO = CK - 1
EPS = 1e-6
IN_COLS = 7168
N_CORES = 8
NSLOT = 6

C_F1 = 0
C_MIX = 8
C_F2 = 16
C_DW = 24
C_DWB = C_DW + 8 * CK
C_LNG = C_DWB + 8
C_LNB = C_LNG + 8
C_QN = C_LNB + 8
C_KN = C_QN + 1
C_SUB = C_KN + 1
C_LQ = C_SUB + 1
C_LK = C_LQ + 128
NCV = C_LK + 128


class Buf:
    __slots__ = ("name", "w", "r")

    def __init__(self, name):
        self.name = name
        self.w = None
        self.r = {}


class DSem:
    __slots__ = ("sem", "count", "last")

    def __init__(self, sem):
        self.sem = sem
        self.count = 0
        self.last = None


class Op:
    __slots__ = ("eng", "fn", "deps", "sig", "sem", "val", "dsem", "ndma")


class Prog:
    ENGS = ("pe", "act", "dve", "pool", "sp")

    def __init__(self):
        self.ops = {e: [] for e in self.ENGS}
        self.dsems = []

    def add(self, eng, fn, reads=(), writes=(), dsem=None, ndma=1):
        op = Op()
        op.eng = eng
        op.fn = fn
        op.dsem = dsem
        op.ndma = ndma
        op.sig = dsem is not None
        op.sem = None
        op.val = 0
        deps = set()
        for b in reads:
            if b.w is not None:
                deps.add(b.w)
        for b in writes:
            if b.w is not None:
                deps.add(b.w)
            for o in b.r.values():
                deps.add(o)
        if dsem is not None and dsem.last is not None:
            deps.add(dsem.last)
        if eng == "pe":
            deps = {d for d in deps if not (d.eng == "pe" and d.dsem is None)}
        op.deps = list(deps)
        for d in op.deps:
            d.sig = True
        rkey = eng if dsem is None else ("dma", id(op))
        for b in reads:
            b.r[rkey] = op
        for b in writes:
            b.w = op
            b.r = {}
        if dsem is not None:
            dsem.last = op
        self.ops[eng].append(op)
        return op

    def emit(self, nc, block, esem):
        for e in self.ENGS:
            cnt = 0
            for op in self.ops[e]:
                if op.dsem is not None:
                    op.dsem.count += 16 * op.ndma
                    op.sem = op.dsem.sem
                    op.val = op.dsem.count
                elif op.sig:
                    cnt += 1
                    op.sem = esem[e]
                    op.val = cnt

        def run(engine, e, final=None):
            known = {}
            for op in self.ops[e]:
                need = {}
                for d in op.deps:
                    k = id(d.sem)
                    if k not in need or need[k][1] < d.val:
                        need[k] = (d.sem, d.val)
                for k, (sem, val) in need.items():
                    if known.get(k, 0) < val:
                        engine.wait_ge(sem, val)
                        known[k] = val
                res = op.fn(engine)
                if op.dsem is not None:
                    if not isinstance(res, (list, tuple)):
                        res = [res]
                    assert len(res) == op.ndma
                    for ins in res:
                        ins.then_inc(op.dsem.sem, 16)
                elif op.sig:
                    res.then_inc(op.sem, 1)
            if final is not None:
                final(engine)

        def final_sp(engine):
            for ds in self.dsems:
                if ds.count > 0:
                    engine.wait_ge(ds.sem, ds.count)

        @block.tensor
        def _(eng):
            run(eng, "pe")

        @block.scalar
        def _(eng):
            run(eng, "act")

        @block.vector
        def _(eng):
            run(eng, "dve")

        @block.gpsimd
        def _(eng):
            run(eng, "pool")

        @block.sync
        def _(eng):
            run(eng, "sp", final_sp)


WNAMES = ["ffn1_w_in", "ffn1_w_out", "w_in", "conv_w_out", "attn_w_out", "w_out", "ffn2_w_in", "ffn2_w_out"]
WSHAPES = {"ffn1_w_in": (D, 2 * DFF), "ffn1_w_out": (DFF, D), "w_in": (D, IN_COLS), "conv_w_out": (D, D),
           "attn_w_out": (D, D), "w_out": (D, D), "ffn2_w_in": (D, 2 * DFF), "ffn2_w_out": (DFF, D)}


def build_nc(S, depth, dbg=None):
    NT = S // T
    nc = bass.Bass("TRN2", target_bir_lowering=False)
    xT_in = nc.dram_tensor("xT", [D, S], F32, kind="ExternalInput").ap()
    cvec_in = nc.dram_tensor("cvec", [128, depth * NCV], F32, kind="ExternalInput").ap()
    tri_in = nc.dram_tensor("tri", [128, 256], F32, kind="ExternalInput").ap()
    W = {}
    for n in WNAMES:
        W[n] = nc.dram_tensor(n, [depth, WSHAPES[n][0], WSHAPES[n][1]], F32, kind="ExternalInput").ap()
    yT_out = nc.dram_tensor("yT", [D, S], F32, kind="ExternalOutput").ap()
    xmid = [nc.dram_tensor(f"xmid{l}", [D, S], F32, kind="Internal").ap() for l in range(max(depth - 1, 1))]
    kts = [nc.dram_tensor(f"kts{l}", [NH, 128, S], BF16, kind="Internal").ap() for l in range(depth)]
    vs = [nc.dram_tensor(f"vs{l}", [NH, S, 128], BF16, kind="Internal").ap() for l in range(depth)]

    P = Prog()
    es = ExitStack()
    with es:
        def sb(name, shape, dt):
            return es.enter_context(nc.sbuf_tensor(name, shape, dt))

        def mksem(name):
            return es.enter_context(nc.semaphore(name))

        def dsem(name):
            d = DSem(mksem(name))
            P.dsems.append(d)
            return d

        esem = {e: mksem("e_" + e) for e in Prog.ENGS}

        xT = sb("xT_sb", [128, KC, T], F32)
        xT_b = [Buf(f"xT{c}") for c in range(KC)]
        xT_ds = [dsem(f"xT_ds{c}") for c in range(KC)]
        hT = sb("hT", [128, KC, T], BF16)
        hT_b = [Buf(f"hT{c}") for c in range(KC)]
        arena = sb("arena", [128, 25 * T], BF16)
        ar_b = [Buf(f"ar{m}") for m in range(25)]
        cT = sb("cT", [128, KC, T], BF16)
        uh = sb("uh", [128, KC, HALO], BF16)
        uh_b = [Buf(f"uh{c}") for c in range(KC)]
        cT_b = [Buf(f"cT{c}") for c in range(KC)]
        qT0 = sb("qT0", [128, NH, T], BF16)
        qT1 = sb("qT1", [128, NH, T], BF16)
        qT_b = [Buf(f"qT{j}") for j in range(NH)]
        aoT = sb("aoT", [128, NH, T], BF16)
        aoT_b = [Buf(f"aoT{j}") for j in range(NH)]
        oc = [sb(f"oc{i}", [128, T], F32) for i in range(2)]
        oc_b = [Buf(f"oc{i}") for i in range(2)]
        rl = [sb(f"rl{i}", [128, T], F32) for i in range(2)]
        rl_b = [Buf(f"rl{i}") for i in range(2)]
        ob = [sb(f"ob{i}", [128, T], F32) for i in range(2)]
        ob_b = [Buf(f"ob{i}") for i in range(2)]
        kn = [sb(f"kn{i}", [128, T], BF16) for i in range(2)]
        kn_b = [Buf(f"kn{i}") for i in range(2)]
        kn_ds = [dsem(f"kn_ds{i}") for i in range(2)]
        vt = [sb(f"vt{i}", [128, D], BF16) for i in range(2)]
        vt_b = [Buf(f"vt{i}") for i in range(2)]
        vt_ds = [dsem(f"vt_ds{i}") for i in range(2)]
        KTj = [sb(f"KTj{i}", [128, S], BF16) for i in range(2)]
        KTj_b = [Buf(f"KTj{i}") for i in range(2)]
        KTj_ds = [dsem(f"KTj_ds{i}") for i in range(2)]
        Vj = [sb(f"Vj{i}", [128, S // 128, 128], BF16) for i in range(2)]
        Vj_b = [Buf(f"Vj{i}") for i in range(2)]
        Vj_ds = [dsem(f"Vj_ds{i}") for i in range(2)]
        NPT = 3
        pt = [sb(f"pt{i}", [128, T], BF16) for i in range(NPT)]
        pt_b = [Buf(f"pt{i}") for i in range(NPT)]
        NTMP = 6
        tmp = [sb(f"tmp{i}", [128, T], F32) for i in range(NTMP)]
        tmp_b = [Buf(f"tmp{i}") for i in range(NTMP)]
        sqb = [sb(f"sqb{i}", [128, T], BF16) for i in range(3)]
        sqb_b = [Buf(f"sqb{i}") for i in range(3)]
        NTB = 3
        tmpb = [sb(f"tmpb{i}", [128, T], BF16) for i in range(NTB)]
        tmpb_b = [Buf(f"tmpb{i}") for i in range(NTB)]
        wslot = [sb(f"wslot{i}", [128, KC * T], BF16) for i in range(NSLOT)]
        wslot_b = [Buf(f"wslot{i}") for i in range(NSLOT)]
        wslot_ds = [dsem(f"wslot_ds{i}") for i in range(NSLOT)]
        cv = sb("cv", [128, depth * NCV], F32)
        cv_b = Buf("cv")
        cv_ds = dsem("cv_ds")
        tri = sb("tri_sb", [128, 256], BF16)
        tri_b = Buf("tri")
        tri_ds = dsem("tri_ds")
        ones_d = sb("ones_d", [128, 128], BF16)
        ones_h = sb("ones_h", [128, 128], BF16)
        ones_1 = sb("ones_1", [128, 128], BF16)
        ones_q = sb("ones_q", [128, 128], BF16)
        const_b = Buf("consts")
        epsc = sb("epsc", [128, 1], F32)
        lay = sb("lay", [128, depth * 8], F32)
        lay_b = Buf("lay")
        lamt = sb("lamt", [128, 128], F32)
        lamt_b = Buf("lamt")
        lam2 = sb("lam2", [128, 4], F32)
        lam2_b = Buf("lam2")

        ps = [es.enter_context(nc.psum_tensor(f"ps{i}", [128, T], F32)) for i in range(8)]
        ps_b = [Buf(f"ps{i}") for i in range(8)]

        class Rot:
            def __init__(self, n):
                self.n = n
                self.i = 0

            def next(self):
                k = self.i % self.n
                self.i += 1
                return k

        tmp_rot = Rot(NTMP)
        tmpb_rot = Rot(NTB)
        slot_rot = Rot(NSLOT)

        P.add("sp", lambda e: e.dma_start(out=cv[:], in_=cvec_in), writes=[cv_b], dsem=cv_ds)
        P.add("pool", lambda e: e.dma_start(out=tri[:], in_=tri_in), writes=[tri_b], dsem=tri_ds)
        P.add("dve", lambda e: e.memset(ones_d[:], 1.0 / D), writes=[const_b])
        P.add("dve", lambda e: e.memset(ones_h[:], 1.0 / 128), writes=[const_b])
        P.add("dve", lambda e: e.memset(ones_1[:], 1.0), writes=[const_b])
        P.add("dve", lambda e: e.memset(ones_q[:], 0.0), writes=[const_b])
        P.add("dve", lambda e: e.memset(ones_q[0:64, 0:64], 1.0 / 64), writes=[const_b])
        P.add("dve", lambda e: e.memset(ones_q[64:128, 64:128], 1.0 / 64), writes=[const_b])
        P.add("dve", lambda e: e.memset(epsc[:], EPS), writes=[const_b])
        P.add("dve", lambda e: e.memset(qT0[:], 0.0), writes=qT_b)
        P.add("dve", lambda e: e.memset(qT1[:], 0.0), writes=qT_b)

        def cvcol(l, col, n=1):
            return cv[:, l * NCV + col: l * NCV + col + n]

        fence = []

        def wload(src_ap, shape):
            k = slot_rot.next()
            a, b = shape
            dst = wslot[k][:, 0:a * b].rearrange("p (a b) -> p a b", a=a)
            rd = list(fence)
            del fence[:]
            P.add("pool", lambda e, dst=dst, src=src_ap: e.dma_start(out=dst, in_=src),
                  reads=rd, writes=[wslot_b[k]], dsem=wslot_ds[k])
            return dst, wslot_b[k]

        def wview(name, l):
            return W[name][l].rearrange("(kc p) n -> p kc n", p=128)

        def mm(out, lhsT, rhs, start, stop, reads, writes):
            P.add("pe", lambda e: e.matmul(out, lhsT=lhsT, rhs=rhs, start=start, stop=stop), reads=reads, writes=writes)

        def rstd_from_mean(ms_ps, ms_buf, n=T):
            k = tmp_rot.next()
            P.add("act", lambda e: e.activation(out=tmp[k][:, 0:n], in_=ms_ps, func=AF.Ln, bias=epsc[:, 0:1], scale=1.0),
                  reads=[ms_buf, const_b], writes=[tmp_b[k]])
            P.add("act", lambda e: e.activation(out=tmp[k][:, 0:n], in_=tmp[k][:, 0:n], func=AF.Exp, scale=-0.5),
                  reads=[tmp_b[k]], writes=[tmp_b[k]])
            return k

        NBANK = 0
        sq_rot = Rot(3)
        norm_pending = []

        def norm_feed(c, lag):
            kb_ = sq_rot.next()
            P.add("act", lambda e: e.activation(out=sqb[kb_][:], in_=xT[:, c, :], func=AF.Square),
                  reads=[xT_b[c]], writes=[sqb_b[kb_]])

            def stats():
                mm(ps[NBANK][:], ones_d[:], sqb[kb_][:], c == 0, c == KC - 1, [sqb_b[kb_], const_b], [ps_b[NBANK]])
            norm_pending.append(stats)
            while len(norm_pending) > lag:
                norm_pending.pop(0)()

        def norm_flush():
            while norm_pending:
                norm_pending.pop(0)()

        def norm_finish(l, gcol):
            norm_flush()
            kr = rstd_from_mean(ps[NBANK][:], ps_b[NBANK])
            for c in range(KC):
                P.add("dve", lambda e, c=c: e.scalar_tensor_tensor(out=hT[:, c, :], in0=xT[:, c, :],
                                                                  scalar=cvcol(l, gcol + c), in1=tmp[kr][:],
                                                                  op0=ALU.mult, op1=ALU.mult),
                      reads=[xT_b[c], tmp_b[kr], cv_b], writes=[hT_b[c]])

        hid = arena[:, 0:MC * T].rearrange("p (m t) -> p m t", m=MC)
        mg = arena[:, 0:KC * T].rearrange("p (m t) -> p m t", m=KC)
        mg_b = ar_b[0:KC]

        def ffn(l, gcol, w_in_name, w_out_name, post=None, feed=True):
            norm_finish(l, gcol)
            wv = wview(w_in_name, l)
            wo = W[w_out_name][l].rearrange("(m p) n -> p m n", p=128)
            ngrp = (MC + 3) // 4
            for g in range(ngrp):
                nm = min(4, MC - 4 * g)
                wa, wa_b = wload(wv[:, :, 512 * g: 512 * g + 128 * nm], (KC, 128 * nm))
                wb, wb_b = wload(wv[:, :, DFF + 512 * g: DFF + 512 * g + 128 * nm], (KC, 128 * nm))
                for mi in range(nm):
                    m = 4 * g + mi
                    ba = (2 * m) % 4
                    bb = (2 * m + 1) % 4
                    for c in range(KC):
                        mm(ps[ba][:], wa[:, c, 128 * mi:128 * (mi + 1)], hT[:, c, :], c == 0, c == KC - 1,
                           [wa_b, hT_b[c]], [ps_b[ba]])
                    for c in range(KC):
                        mm(ps[bb][:], wb[:, c, 128 * mi:128 * (mi + 1)], hT[:, c, :], c == 0, c == KC - 1,
                           [wb_b, hT_b[c]], [ps_b[bb]])
                    k = tmp_rot.next()
                    P.add("act", lambda e, k=k, ba=ba: e.activation(out=tmp[k][:], in_=ps[ba][:], func=AF.Silu),
                          reads=[ps_b[ba]], writes=[tmp_b[k]])
                    P.add("dve", lambda e, k=k, bb=bb, m=m: e.tensor_tensor(out=hid[:, m, :], in0=ps[bb][:], in1=tmp[k][:],
                                                                           op=ALU.mult),
                          reads=[ps_b[bb], tmp_b[k]], writes=[ar_b[m]])
            for d in range(KC):
                wd, wd_b = wload(wo[:, :, 128 * d:128 * (d + 1)], (MC, 128))
                bk = 4 + (d % 4)
                for m in range(MC):
                    mm(ps[bk][:], wd[:, m, :], hid[:, m, :], m == 0, m == MC - 1, [wd_b, ar_b[m]], [ps_b[bk]])
                P.add("dve", lambda e, d=d, bk=bk: e.scalar_tensor_tensor(out=xT[:, d, :], in0=ps[bk][:], scalar=0.5,
                                                                         in1=xT[:, d, :], op0=ALU.mult, op1=ALU.add),
                      reads=[ps_b[bk], xT_b[d]], writes=[xT_b[d]])
                if post is not None:
                    post(d)
                if feed:
                    norm_feed(d, 2)

        UW = 544
        u_all = arena[:, 0:KC * UW].rearrange("p (c w) -> p c w", c=KC)
        Y0 = 9 * T
        y_all = arena[:, Y0:Y0 + 2 * KC * T].bitcast(F32).rearrange("p (c t) -> p c t", c=KC)

        def u_bufs(c):
            lo = (c * UW * 2) // 1024
            hi = ((c + 1) * UW * 2 - 1) // 1024
            return [ar_b[i] for i in range(lo, hi + 1)]

        def y_bufs(c):
            return [ar_b[9 + 2 * c], ar_b[10 + 2 * c]]

        def layer_setup(l):
            lam_init = 0.8 - 0.6 * float(np.exp(-0.3 * l))
            P.add("dve", lambda e: e.tensor_single_scalar(out=lay[:, l * 8 + 0:l * 8 + 1], in_=cvcol(l, C_QN), scalar=0.125, op=ALU.mult),
                  reads=[cv_b], writes=[lay_b])
            P.add("dve", lambda e: e.tensor_single_scalar(out=lay[:, l * 8 + 1:l * 8 + 2], in_=cvcol(l, C_SUB), scalar=1.0 - lam_init, op=ALU.mult),
                  reads=[cv_b], writes=[lay_b])
            P.add("dve", lambda e: e.tensor_tensor(out=lamt[:], in0=cvcol(l, C_LQ, 128), in1=cvcol(l, C_LK, 128), op=ALU.mult),
                  reads=[cv_b], writes=[lamt_b])
            P.add("dve", lambda e: e.reduce_sum(out=lam2[:, 0:2], in_=lamt[:].rearrange("p (a b) -> p a b", a=2),
                                                axis=mybir.AxisListType.X),
                  reads=[lamt_b], writes=[lam2_b])
            P.add("act", lambda e: e.activation(out=lam2[:, 2:4], in_=lam2[:, 0:2], func=AF.Exp), reads=[lam2_b], writes=[lam2_b])
            P.add("dve", lambda e: e.scalar_tensor_tensor(out=lay[:, l * 8 + 2:l * 8 + 3], in0=lam2[:, 3:4], scalar=-lam_init,
                                                         in1=lam2[:, 2:3], op0=ALU.add, op1=ALU.subtract),
                  reads=[lam2_b], writes=[lay_b])

        def mixer(l, i):
            wv = wview("w_in", l)
            norm_finish(l, C_MIX)
            tok0 = i * T
            wvb = [wload(wv[:, :, 4096 + 512 * f: 4096 + 512 * (f + 1)], (KC, 512)) for f in range(2)]
            for tc in range(T // 128):
                r = tc % 2
                for f in range(2):
                    bk = (2 * tc + f) % 4
                    for c in range(KC):
                        mm(ps[bk][:], hT[:, c, 128 * tc:128 * (tc + 1)], wvb[f][0][:, c, :], c == 0, c == KC - 1,
                           [hT_b[c], wvb[f][1]], [ps_b[bk]])
                    P.add("act", lambda e, r=r, f=f, bk=bk: e.activation(out=vt[r][:, 512 * f:512 * (f + 1)], in_=ps[bk][:], func=AF.Copy),
                          reads=[ps_b[bk]], writes=[vt_b[r]])
                dst = vs[l][:, tok0 + 128 * tc: tok0 + 128 * (tc + 1), :].rearrange("h t e -> t h e")
                src = vt[r][:].rearrange("p (h e) -> p h e", h=NH)
                P.add("sp", lambda e, dst=dst, src=src: e.dma_start(out=dst, in_=src), reads=[vt_b[r]],
                      writes=[vs_b[l][i]], dsem=vt_ds[r])
            pending = None
            for which in ("k", "q"):
                base = 3072 if which == "k" else 2048
                blocks = [wload(wv[:, :, base + 512 * f: base + 512 * (f + 1)], (KC, 512)) for f in range(2)]
                for j in range(NH):
                    wblk, wblk_b = blocks[j // 4]
                    bk = j % 4
                    for c in range(KC):
                        mm(ps[bk][:], wblk[:, c, 128 * (j % 4):128 * (j % 4 + 1)], hT[:, c, :], c == 0, c == KC - 1,
                           [wblk_b, hT_b[c]], [ps_b[bk]])
                    ksq = tmpb_rot.next()
                    P.add("act", lambda e, ksq=ksq, bk=bk: e.activation(out=tmpb[ksq][:], in_=ps[bk][:], func=AF.Square),
                          reads=[ps_b[bk]], writes=[tmpb_b[ksq]])

                    def tail(which=which, j=j, bk=bk, ksq=ksq):
                        b2 = 4 + (j % 2)
                        mm(ps[b2][:], ones_q[:], tmpb[ksq][:], True, True, [tmpb_b[ksq], const_b], [ps_b[b2]])
                        kr = rstd_from_mean(ps[b2][:], ps_b[b2])
                        if which == "k":
                            r = j % 2
                            P.add("dve", lambda e: e.scalar_tensor_tensor(
                                out=kn[r][:], in0=ps[bk][:], scalar=cvcol(l, C_KN), in1=tmp[kr][:], op0=ALU.mult, op1=ALU.mult),
                                reads=[ps_b[bk], tmp_b[kr], cv_b], writes=[kn_b[r]])
                            P.add("sp", lambda e: e.dma_start(out=kts[l][j, :, tok0:tok0 + T], in_=kn[r][:]),
                                  reads=[kn_b[r]], writes=[kts_b[l][j][i]], dsem=kn_ds[r])
                        else:
                            gq = lay[:, l * 8:l * 8 + 1]
                            P.add("dve", lambda e: e.scalar_tensor_tensor(
                                out=qT0[0:64, j, :], in0=ps[bk][0:64, :], scalar=gq[0:64, :], in1=tmp[kr][0:64, :],
                                op0=ALU.mult, op1=ALU.mult),
                                reads=[ps_b[bk], tmp_b[kr], lay_b], writes=[qT_b[j]])
                            P.add("dve", lambda e: e.scalar_tensor_tensor(
                                out=qT1[64:128, j, :], in0=ps[bk][64:128, :], scalar=gq[64:128, :], in1=tmp[kr][64:128, :],
                                op0=ALU.mult, op1=ALU.mult),
                                reads=[ps_b[bk], tmp_b[kr], lay_b], writes=[qT_b[j]])
                    if pending is not None:
                        pending()
                    pending = tail
            if pending is not None:
                pending()
                pending = None
            if i == 0:
                P.add("dve", lambda e: e.memset(u_all[:, :, 0:HALO], 0.0), writes=[ar_b[k_] for k_ in range(9)])
            else:
                for cc in range(KC):
                    P.add("act", lambda e, cc=cc: e.activation(out=u_all[:, cc, 0:HALO], in_=uh[:, cc, :], func=AF.Copy),
                          reads=[uh_b[cc]], writes=u_bufs(cc))
            ablk = [wload(wv[:, :, 512 * f: 512 * (f + 1)], (KC, 512)) for f in range(2)]
            gblk = [wload(wv[:, :, 1024 + 512 * f: 1024 + 512 * (f + 1)], (KC, 512)) for f in range(2)]
            for cc in range(KC):
                ba = (2 * cc) % 4
                bg = (2 * cc + 1) % 4
                wa, wa_b = ablk[cc // 4]
                wg, wg_b = gblk[cc // 4]
                for c in range(KC):
                    mm(ps[bg][:], wg[:, c, 128 * (cc % 4):128 * (cc % 4 + 1)], hT[:, c, :], c == 0, c == KC - 1,
                       [wg_b, hT_b[c]], [ps_b[bg]])
                for c in range(KC):
                    mm(ps[ba][:], wa[:, c, 128 * (cc % 4):128 * (cc % 4 + 1)], hT[:, c, :], c == 0, c == KC - 1,
                       [wa_b, hT_b[c]], [ps_b[ba]])
                k = tmp_rot.next()
                P.add("act", lambda e, k=k, bg=bg: e.activation(out=tmp[k][:], in_=ps[bg][:], func=AF.Sigmoid),
                      reads=[ps_b[bg]], writes=[tmp_b[k]])
                P.add("dve", lambda e, k=k, ba=ba, cc=cc: e.tensor_tensor(out=u_all[:, cc, HALO:HALO + T], in0=ps[ba][:], in1=tmp[k][:],
                                                                         op=ALU.mult),
                      reads=[ps_b[ba], tmp_b[k]], writes=u_bufs(cc))

            nkb = 4 * (i + 1)
            L = nkb * 128
            cg = conv_gen(l)
            NCONV_HEADS = 7
            quota = (KC * CK + NCONV_HEADS - 1) // NCONV_HEADS

            def pump(k):
                for _ in range(k):
                    if next(cg, "done") == "done":
                        return

            heads = [make_head(l, i, j, nkb, L) for j in range(NH)]
            nb = 2 * nkb
            carry = None
            heads[0]["s_mm"](0)
            heads[0]["s_mm"](1)
            for j in range(NH):
                h = heads[j]
                for n in range(nb):
                    if n + 2 < nb:
                        h["s_mm"](n + 2)
                    elif j + 1 < NH:
                        heads[j + 1]["s_mm"](n + 2 - nb)
                    h["pv_mm"](n)
                    if j < NCONV_HEADS:
                        pump((quota * (n + 1)) // nb - (quota * n) // nb)
                    if n == 5 and carry is not None:
                        carry((n + 3) % 4)
                        carry = None
                if carry is not None:
                    carry(0)
                carry = h["finalize"]()
                if j == NCONV_HEADS - 1:
                    pump(KC * CK + 8)
                    conv_ln(l)
            carry(0)
            merge_and_out(l)

        def make_head(l, i, j, nkb, L):
            s = j % 2
            base = j * 2 * nkb
            blocks = [(kb, c) for kb in range(nkb) for c in range(2)]
            OB = [4, 5]
            LB = [6, 7]
            loaded = []

            def load():
                P.add("sp", lambda e: e.dma_start(out=KTj[s][:, 0:L], in_=kts[l][j, :, 0:L]),
                      reads=[kts_b[l][j][t_] for t_ in range(i + 1)], writes=[KTj_b[s]], dsem=KTj_ds[s])
                P.add("sp", lambda e: e.dma_start(out=Vj[s][:, 0:nkb, :], in_=vs[l][j, 0:L, :].rearrange("(n p) e -> p n e", p=128)),
                      reads=[vs_b[l][t_] for t_ in range(i + 1)], writes=[Vj_b[s]], dsem=Vj_ds[s])

            def s_mm(n):
                if not loaded:
                    load()
                    loaded.append(1)
                kb, c = blocks[n]
                g = base + n
                r = kb - 4 * i
                c0 = 128 * r if r > 0 else 0
                sbk = g % 4
                q = (qT0 if c == 0 else qT1)
                mm(ps[sbk][:, c0:T], KTj[s][:, 128 * kb:128 * (kb + 1)], q[:, j, c0:T], True, r < 0,
                   [KTj_b[s], qT_b[j]], [ps_b[sbk]])
                if r >= 0:
                    mm(ps[sbk][:, c0:c0 + 128], tri[:, 128:256], tri[:, 0:128], False, True, [tri_b], [ps_b[sbk]])
                p_ = g % NPT
                P.add("act", lambda e: e.activation(out=pt[p_][:, c0:T], in_=ps[sbk][:, c0:T], func=AF.Exp),
                      reads=[ps_b[sbk]], writes=[pt_b[p_]])

            def pv_mm(n):
                kb, c = blocks[n]
                g = base + n
                r = kb - 4 * i
                c0 = 128 * r if r > 0 else 0
                p_ = g % NPT
                mm(ps[OB[c]][:, c0:T], Vj[s][:, kb, :], pt[p_][:, c0:T], kb == 0, kb == nkb - 1,
                   [Vj_b[s], pt_b[p_]], [ps_b[OB[c]]])
                mm(ps[LB[c]][:, c0:T], ones_1[:], pt[p_][:, c0:T], kb == 0, kb == nkb - 1,
                   [const_b, pt_b[p_]], [ps_b[LB[c]]])

            def finalize():
                for c in range(2):
                    P.add("act", lambda e, c=c: e.activation(out=rl[c][:], in_=ps[LB[c]][:], func=AF.Ln),
                          reads=[ps_b[LB[c]]], writes=[rl_b[c]])
                    P.add("dve", lambda e, c=c: e.tensor_copy(out=oc[c][:], in_=ps[OB[c]][:]),
                          reads=[ps_b[OB[c]]], writes=[oc_b[c]])
                a = []
                for c in range(2):
                    P.add("act", lambda e, c=c: e.activation(out=rl[c][:], in_=rl[c][:], func=AF.Exp, scale=-1.0),
                          reads=[rl_b[c]], writes=[rl_b[c]])
                    ka = tmp_rot.next()
                    P.add("dve", lambda e, ka=ka, c=c: e.tensor_tensor(out=tmp[ka][:], in0=oc[c][:], in1=rl[c][:], op=ALU.mult),
                          reads=[oc_b[c], rl_b[c]], writes=[tmp_b[ka]])
                    a.append(ka)
                ko = j % 2
                P.add("dve", lambda e: e.scalar_tensor_tensor(out=ob[ko][:], in0=tmp[a[1]][:], scalar=lay[:, l * 8 + 2:l * 8 + 3],
                                                             in1=tmp[a[0]][:], op0=ALU.mult, op1=ALU.add),
                      reads=[tmp_b[a[0]], tmp_b[a[1]], lay_b], writes=[ob_b[ko]])

                def fin2(sbk):
                    ksq = tmpb_rot.next()
                    P.add("act", lambda e: e.activation(out=tmpb[ksq][:], in_=ob[ko][:], func=AF.Square),
                          reads=[ob_b[ko]], writes=[tmpb_b[ksq]])
                    mm(ps[sbk][:], ones_h[:], tmpb[ksq][:], True, True, [tmpb_b[ksq], const_b], [ps_b[sbk]])
                    kr = rstd_from_mean(ps[sbk][:], ps_b[sbk])
                    P.add("dve", lambda e: e.scalar_tensor_tensor(out=aoT[:, j, :], in0=ob[ko][:], scalar=lay[:, l * 8 + 1:l * 8 + 2],
                                                                 in1=tmp[kr][:], op0=ALU.mult, op1=ALU.mult),
                          reads=[ob_b[ko], tmp_b[kr], lay_b], writes=[aoT_b[j]])
                return fin2

            return {"s_mm": s_mm, "pv_mm": pv_mm, "finalize": finalize}

        def conv_gen(l):
            for c0 in range(0, KC, 2):
                for k in range(CK):
                    for c in (c0, c0 + 1):
                        yb = y_bufs(c)
                        ub = u_bufs(c)
                        if k == 0:
                            P.add("dve", lambda e, c=c: e.tensor_scalar(out=y_all[:, c, :], in0=u_all[:, c, 0:T],
                                                                       scalar1=cvcol(l, C_DW + c * CK), scalar2=cvcol(l, C_DWB + c),
                                                                       op0=ALU.mult, op1=ALU.add),
                                  reads=ub + [cv_b], writes=yb)
                        else:
                            P.add("dve", lambda e, c=c, k=k: e.scalar_tensor_tensor(out=y_all[:, c, :], in0=u_all[:, c, k:k + T],
                                                                              scalar=cvcol(l, C_DW + c * CK + k), in1=y_all[:, c, :],
                                                                              op0=ALU.mult, op1=ALU.add),
                                  reads=ub + yb + [cv_b], writes=yb)
                        yield
                for c in (c0, c0 + 1):
                    P.add("act", lambda e, c=c: e.activation(out=uh[:, c, :], in_=u_all[:, c, T:T + HALO], func=AF.Copy),
                          reads=u_bufs(c), writes=[uh_b[c]])

        def conv_ln(l):
            for c in range(KC):
                k1 = tmpb_rot.next()
                k2 = tmpb_rot.next()
                P.add("act", lambda e, c=c, k1=k1: e.activation(out=tmpb[k1][:], in_=y_all[:, c, :], func=AF.Copy),
                      reads=y_bufs(c), writes=[tmpb_b[k1]])
                P.add("act", lambda e, c=c, k2=k2: e.activation(out=tmpb[k2][:], in_=y_all[:, c, :], func=AF.Square),
                      reads=y_bufs(c), writes=[tmpb_b[k2]])
                mm(ps[0][:], ones_d[:], tmpb[k1][:], c == 0, c == KC - 1, [tmpb_b[k1], const_b], [ps_b[0]])
                mm(ps[1][:], ones_d[:], tmpb[k2][:], c == 0, c == KC - 1, [tmpb_b[k2], const_b], [ps_b[1]])
            kmean = tmp_rot.next()
            kvar = tmp_rot.next()
            P.add("act", lambda e: e.activation(out=tmp[kmean][:], in_=ps[0][:], func=AF.Copy), reads=[ps_b[0]], writes=[tmp_b[kmean]])
            P.add("dve", lambda e: e.tensor_tensor(out=tmp[kvar][:], in0=tmp[kmean][:], in1=tmp[kmean][:], op=ALU.mult),
                  reads=[tmp_b[kmean]], writes=[tmp_b[kvar]])
            P.add("dve", lambda e: e.tensor_tensor(out=tmp[kvar][:], in0=ps[1][:], in1=tmp[kvar][:], op=ALU.subtract),
                  reads=[ps_b[1], tmp_b[kvar]], writes=[tmp_b[kvar]])
            P.add("dve", lambda e: e.tensor_single_scalar(out=tmp[kvar][:], in_=tmp[kvar][:], scalar=0.0, op=ALU.max),
                  reads=[tmp_b[kvar]], writes=[tmp_b[kvar]])
            kr = rstd_from_mean(tmp[kvar][:], tmp_b[kvar])
            knm = tmp_rot.next()
            P.add("dve", lambda e: e.scalar_tensor_tensor(out=tmp[knm][:], in0=tmp[kmean][:], scalar=-1.0, in1=tmp[kr][:],
                                                         op0=ALU.mult, op1=ALU.mult),
                  reads=[tmp_b[kmean], tmp_b[kr]], writes=[tmp_b[knm]])
            for c in range(KC):
                P.add("dve", lambda e, c=c: e.tensor_tensor(out=y_all[:, c, :], in0=y_all[:, c, :], in1=tmp[kr][:], op=ALU.mult),
                      reads=y_bufs(c) + [tmp_b[kr]], writes=y_bufs(c))
                P.add("dve", lambda e, c=c: e.tensor_tensor(out=y_all[:, c, :], in0=y_all[:, c, :], in1=tmp[knm][:], op=ALU.add),
                      reads=y_bufs(c) + [tmp_b[knm]], writes=y_bufs(c))
                P.add("act", lambda e, c=c: e.activation(out=cT[:, c, :], in_=y_all[:, c, :], func=AF.Silu,
                                                        bias=cvcol(l, C_LNB + c), scale=cvcol(l, C_LNG + c)),
                      reads=y_bufs(c) + [cv_b], writes=[cT_b[c]])

        def merge_and_out(l):
            wv = wview("w_in", l)
            wc_v = wview("conv_w_out", l)
            wa_v = wview("attn_w_out", l)
            wo_v = wview("w_out", l)
            for f in range(2):
                gcb, gcb_b = wload(wv[:, :, 5120 + 512 * f: 5120 + 512 * (f + 1)], (KC, 512))
                wcb, wcb_b = wload(wc_v[:, :, 512 * f:512 * (f + 1)], (KC, 512))
                gab, gab_b = wload(wv[:, :, 6144 + 512 * f: 6144 + 512 * (f + 1)], (KC, 512))
                wab, wab_b = wload(wa_v[:, :, 512 * f:512 * (f + 1)], (KC, 512))
                for dd in range(4):
                    d = 4 * f + dd
                    cs = slice(128 * dd, 128 * (dd + 1))
                    for c in range(KC):
                        mm(ps[0][:], gcb[:, c, cs], hT[:, c, :], c == 0, c == KC - 1, [gcb_b, hT_b[c]], [ps_b[0]])
                    for c in range(KC):
                        mm(ps[1][:], wcb[:, c, cs], cT[:, c, :], c == 0, c == KC - 1, [wcb_b, cT_b[c]], [ps_b[1]])
                    for c in range(KC):
                        mm(ps[2][:], gab[:, c, cs], hT[:, c, :], c == 0, c == KC - 1, [gab_b, hT_b[c]], [ps_b[2]])
                    for c in range(KC):
                        mm(ps[3][:], wab[:, c, cs], aoT[:, c, :], c == 0, c == KC - 1, [wab_b, aoT_b[c]], [ps_b[3]])
                    k1 = tmp_rot.next()
                    k2 = tmp_rot.next()
                    P.add("act", lambda e, k1=k1: e.activation(out=tmp[k1][:], in_=ps[0][:], func=AF.Sigmoid),
                          reads=[ps_b[0]], writes=[tmp_b[k1]])
                    P.add("act", lambda e, k2=k2: e.activation(out=tmp[k2][:], in_=ps[2][:], func=AF.Sigmoid),
                          reads=[ps_b[2]], writes=[tmp_b[k2]])
                    P.add("dve", lambda e, k1=k1: e.tensor_tensor(out=tmp[k1][:], in0=ps[1][:], in1=tmp[k1][:], op=ALU.mult),
                          reads=[ps_b[1], tmp_b[k1]], writes=[tmp_b[k1]])
                    P.add("dve", lambda e, k2=k2: e.tensor_tensor(out=tmp[k2][:], in0=ps[3][:], in1=tmp[k2][:], op=ALU.mult),
                          reads=[ps_b[3], tmp_b[k2]], writes=[tmp_b[k2]])
                    P.add("dve", lambda e, k1=k1, k2=k2, d=d: e.tensor_tensor(out=mg[:, d, :], in0=tmp[k1][:], in1=tmp[k2][:], op=ALU.add),
                          reads=[tmp_b[k1], tmp_b[k2]], writes=[mg_b[d]])
            for f in range(2):
                wob, wob_b = wload(wo_v[:, :, 512 * f:512 * (f + 1)], (KC, 512))
                for dd in range(4):
                    d = 4 * f + dd
                    bk = 4 + dd
                    for c in range(KC):
                        mm(ps[bk][:], wob[:, c, 128 * dd:128 * (dd + 1)], mg[:, c, :], c == 0, c == KC - 1,
                           [wob_b, mg_b[c]], [ps_b[bk]])
                    P.add("dve", lambda e, d=d, bk=bk: e.tensor_tensor(out=xT[:, d, :], in0=ps[bk][:], in1=xT[:, d, :], op=ALU.add),
                          reads=[ps_b[bk], xT_b[d]], writes=[xT_b[d]])
                    norm_feed(d, 2)

        kts_b = [[[Buf(f"kts{l}_{j}_{i}") for i in range(NT)] for j in range(NH)] for l in range(depth)]
        vs_b = [[Buf(f"vs{l}_{i}") for i in range(NT)] for l in range(depth)]
        xm_b = [[[Buf(f"xm{l}_{i}_{c}") for c in range(KC)] for i in range(NT)] for l in range(depth)]

        def x_view(l, i, is_dst):
            t_ = (yT_out if l == depth - 1 else xmid[l]) if is_dst else (xT_in if l == 0 else xmid[l - 1])
            return t_.rearrange("(c p) t -> p c t", p=128)[:, :, i * T:(i + 1) * T]

        def x_load(l, i, c):
            sv = x_view(l, i, False)
            rd = [xm_b[l - 1][i][c]] if l > 0 else []
            P.add("sp", lambda e: e.dma_start(out=xT[:, c, :], in_=sv[:, c, :]), reads=rd, writes=[xT_b[c]], dsem=xT_ds[c])

        def x_store(l, i, c):
            dv = x_view(l, i, True)
            P.add("sp", lambda e: e.dma_start(out=dv[:, c, :], in_=xT[:, c, :]), reads=[xT_b[c]], writes=[xm_b[l][i][c]],
                  dsem=xT_ds[c])

        seq = [(l, i) for l in range(depth) for i in range(NT)]
        for c in range(KC):
            x_load(0, 0, c)
            norm_feed(c, 2)
        for n, (l, i) in enumerate(seq):
            if i == 0:
                layer_setup(l)
            ffn(l, C_F1, "ffn1_w_in", "ffn1_w_out")
            mixer(l, i)

            def post(d, l=l, i=i, n=n):
                x_store(l, i, d)
                if n + 1 < len(seq):
                    x_load(seq[n + 1][0], seq[n + 1][1], d)
                    if d == KC - 1:
                        fence.append(xT_b[d])

            ffn(l, C_F2, "ffn2_w_in", "ffn2_w_out", post=post, feed=(n + 1 < len(seq)))

        if dbg is not None:
            dbg["sbuf_remaining"] = nc.sbuf_bytes_remaining
            dbg["nops"] = {e: len(P.ops[e]) for e in Prog.ENGS}
        block = es.enter_context(nc.Block())
        P.emit(nc, block, esem)
    return nc


def make_cvec(inp, depth):
    cv = np.zeros((128, depth * NCV), np.float32)
    for l in range(depth):
        o = l * NCV

        def pc(v):
            return np.ascontiguousarray(np.asarray(v, np.float32).reshape(KC, 128).T)

        cv[:, o + C_F1:o + C_F1 + 8] = pc(inp["ffn1_norm"][l])
        cv[:, o + C_MIX:o + C_MIX + 8] = pc(inp["mix_norm"][l])
        cv[:, o + C_F2:o + C_F2 + 8] = pc(inp["ffn2_norm"][l])
        dw = np.asarray(inp["conv_dw"][l], np.float32)
        for c in range(KC):
            cv[:, o + C_DW + c * CK:o + C_DW + (c + 1) * CK] = dw[:, c * 128:(c + 1) * 128].T
        cv[:, o + C_DWB:o + C_DWB + 8] = pc(inp["conv_dw_b"][l])
        cv[:, o + C_LNG:o + C_LNG + 8] = pc(inp["conv_ln_g"][l])
        cv[:, o + C_LNB:o + C_LNB + 8] = pc(inp["conv_ln_b"][l])
        cv[:, o + C_QN] = np.tile(np.asarray(inp["q_norm"][l], np.float32), 2)
        cv[:, o + C_KN] = np.tile(np.asarray(inp["k_norm"][l], np.float32), 2)
        cv[:, o + C_SUB] = np.asarray(inp["attn_subln"][l], np.float32)
        cv[:, o + C_LQ:o + C_LQ + 128] = np.asarray(inp["lam_q"][l], np.float32).reshape(1, 128)
        cv[:, o + C_LK:o + C_LK + 128] = np.asarray(inp["lam_k"][l], np.float32).reshape(1, 128)
    return cv


def make_tri():
    k = np.arange(128)[:, None]
    q = np.arange(128)[None, :]
    neg = np.where(q < k, -100.0, 0.0).astype(np.float32)
    return np.concatenate([neg, np.eye(128, dtype=np.float32)], axis=1)


_NC_CACHE = {}


def run(inputs, S, depth, n_cores, trace=False):
    key = (S, depth)
    if key not in _NC_CACHE:
        _NC_CACHE[key] = build_nc(S, depth)
    nc = _NC_CACHE[key]
    x = np.asarray(inputs["x"], np.float32)
    cvec = make_cvec(inputs, depth)
    tri = make_tri()
    wts = {n: np.ascontiguousarray(np.asarray(inputs[n], np.float32)[:depth]) for n in WNAMES}
    in_maps = []
    for b in range(n_cores):
        m = {"xT": np.ascontiguousarray(x[b, :S].T), "cvec": cvec, "tri": tri}
        m.update(wts)
        in_maps.append(m)
    res = run_bass_kernel_spmd(nc, in_maps, core_ids=list(range(n_cores)), trace=trace)
    out = np.stack([np.ascontiguousarray(r["yT"].T) for r in res.results], axis=0)
    return out.astype(np.float32), res


def kernel(**inputs):
    out, _ = run(inputs, 4096, 2, N_CORES)
    return out
```
